# Optimizing a Trainium2 kernel written in Bass

```python
import jax, jax.numpy as jnp
from jax import lax
import numpy as np

D_MODEL = 1024
BATCH = 16
SEQ = 4096
DEPTH = 1

PLE_DIM = 256
CHUNK = 128
RET_HEADS = 4
RET_V_DIM = D_MODEL // RET_HEADS
RET_QK_DIM = RET_V_DIM // 2
RET_WIDTH = RET_HEADS * RET_V_DIM
SGU_GROUPS = 4
SGU_WIDTH = D_MODEL
SGU_GROUP_DIM = SGU_WIDTH // SGU_GROUPS
ROPE_BASE = 10000.0
NORM_EPS = 1e-6
GN_EPS = 1e-5
IN_SPLITS = (RET_HEADS * RET_QK_DIM, RET_HEADS * RET_QK_DIM, RET_WIDTH, RET_WIDTH,
             SGU_WIDTH, SGU_WIDTH, SGU_WIDTH, D_MODEL, D_MODEL)
IN_WIDTH = sum(IN_SPLITS)

kernel_name = 'hybrid_retention_sgu_block'


def rms_norm(x, g):
    xf = x.astype(jnp.float32)
    y = xf * lax.rsqrt(jnp.mean(xf * xf, axis=-1, keepdims=True) + NORM_EPS)
    return (y * g.astype(jnp.float32)).astype(x.dtype)


def unit_norm(x, eps):
    xf = x.astype(jnp.float32)
    mu = jnp.mean(xf, axis=-1, keepdims=True)
    var = jnp.mean(jnp.square(xf - mu), axis=-1, keepdims=True)
    return ((xf - mu) * lax.rsqrt(var + eps)).astype(x.dtype)


def rotary(x):
    s, d = x.shape[1], x.shape[-1]
    half = d // 2
    inv = ROPE_BASE ** (-jnp.arange(half, dtype=jnp.float32) / half)
    ang = jnp.arange(s, dtype=jnp.float32)[:, None] * inv[None, :]
    cos = jnp.cos(ang)[None, :, None, :].astype(x.dtype)
    sin = jnp.sin(ang)[None, :, None, :].astype(x.dtype)
    x1, x2 = x[..., :half], x[..., half:]
    return jnp.concatenate([x1 * cos - x2 * sin, x2 * cos + x1 * sin], axis=-1)


def retention(q, k, v):
    b, s, h, dk = q.shape
    dv = v.shape[-1]
    n = s // CHUNK
    log_g = jnp.log(1.0 - 2.0 ** (-5.0 - jnp.arange(h, dtype=jnp.float32)))
    idx = jnp.arange(CHUNK, dtype=jnp.float32)
    diff = idx[:, None] - idx[None, :]
    decay_in = jnp.where(diff[None] >= 0,
                         jnp.exp(jnp.maximum(diff, 0.0)[None] * log_g[:, None, None]),
                         0.0).astype(q.dtype)
    zeta = jnp.exp((CHUNK - 1.0 - idx)[:, None] * log_g[None, :]).astype(q.dtype)
    xi = jnp.exp((idx + 1.0)[:, None] * log_g[None, :]).astype(q.dtype)
    chunk_decay = jnp.exp(CHUNK * log_g).astype(v.dtype)

    qc = (q * (dk ** -0.5)).reshape(b, n, CHUNK, h, dk)
    kc = k.reshape(b, n, CHUNK, h, dk)
    vc = v.reshape(b, n, CHUNK, h, dv)

    scores = jnp.einsum('bnihd,bnjhd->bnhij', qc, kc) * decay_in
    inner = jnp.einsum('bnhij,bnjhe->bnihe', scores, vc)

    kv = jnp.einsum('bnjhd,bnjhe->nbhde', kc * zeta[:, :, None], vc)

    def step(state, kv_n):
        return kv_n + chunk_decay[None, :, None, None] * state, state

    _, prev = lax.scan(step, jnp.zeros_like(kv[0]), kv)
    cross = jnp.einsum('bnihd,nbhde->bnihe', qc * xi[:, :, None], prev)
    return (inner + cross).reshape(b, s, h, dv)


def spatial_gating(u, v, ws, bs):
    b, s, _ = u.shape
    n = s // CHUNK
    v = unit_norm(v, GN_EPS).reshape(b, n, CHUNK, SGU_GROUPS, SGU_GROUP_DIM)
    ws_causal = ws * jnp.tril(jnp.ones((CHUNK, CHUNK), ws.dtype))[None]
    mixed = jnp.einsum('gij,bnjgd->bnigd', ws_causal, v) + bs.T[None, None, :, :, None]
    return u * mixed.reshape(b, s, SGU_WIDTH)


def setup_inputs(seed: int = 0) -> dict:
    key = jax.random.key(seed)
    ks = jax.random.split(key, 13)
    f32 = jnp.float32
    nrm = lambda k, shape, scale: jax.random.normal(k, shape, f32) * scale
    return {
        'x': jax.random.normal(ks[0], (BATCH, SEQ, D_MODEL), f32),
        'p': jax.random.normal(ks[1], (DEPTH, BATCH, SEQ, PLE_DIM), f32),
        'w_in': nrm(ks[2], (DEPTH, D_MODEL, IN_WIDTH), D_MODEL ** -0.5),
        'w_ret_out': nrm(ks[3], (DEPTH, RET_WIDTH, D_MODEL), RET_WIDTH ** -0.5),
        'w_sgu_out': nrm(ks[4], (DEPTH, SGU_WIDTH, D_MODEL), SGU_WIDTH ** -0.5),
        'w_out': nrm(ks[5], (DEPTH, D_MODEL, D_MODEL), D_MODEL ** -0.5),
        'sgu_ws': nrm(ks[6], (DEPTH, SGU_GROUPS, CHUNK, CHUNK), CHUNK ** -0.5),
        'sgu_bs': 1.0 + nrm(ks[7], (DEPTH, SGU_GROUPS, CHUNK), 0.01),
        'w_ple_gate': nrm(ks[8], (DEPTH, D_MODEL, D_MODEL), D_MODEL ** -0.5),
        'w_ple_proj': nrm(ks[9], (DEPTH, PLE_DIM, D_MODEL), PLE_DIM ** -0.5),
        'g_mixer': 1.0 + nrm(ks[10], (DEPTH, D_MODEL), 0.05),
        'g_ple': 1.0 + nrm(ks[11], (DEPTH, D_MODEL), 0.05),
        'g_final': 1.0 + nrm(ks[12], (D_MODEL,), 0.05),
    }


def reference(x, p, w_in, w_ret_out, w_sgu_out, w_out, sgu_ws, sgu_bs,
              w_ple_gate, w_ple_proj, g_mixer, g_ple, g_final):
    b, s, _ = x.shape
    split_at = np.cumsum(IN_SPLITS)[:-1].tolist()
    for i in range(DEPTH):
        h = rms_norm(x, g_mixer[i])
        proj = jnp.einsum('bsd,de->bse', h, w_in[i])
        q, k, v, ret_gate, su, sv, sgu_gate, merge_ret, merge_sgu = jnp.split(proj, split_at, axis=-1)

        q = rotary(q.reshape(b, s, RET_HEADS, RET_QK_DIM))
        k = rotary(k.reshape(b, s, RET_HEADS, RET_QK_DIM))
        ret = retention(q, k, v.reshape(b, s, RET_HEADS, RET_V_DIM))
        ret = unit_norm(ret, GN_EPS).reshape(b, s, RET_WIDTH) * jax.nn.silu(ret_gate)

        sgu = spatial_gating(jax.nn.gelu(su, approximate=False), jax.nn.gelu(sv, approximate=False),
                             sgu_ws[i], sgu_bs[i]) * jax.nn.silu(sgu_gate)

        merged = (jax.nn.sigmoid(merge_ret) * jnp.einsum('bse,ed->bsd', ret, w_ret_out[i])
                  + jax.nn.sigmoid(merge_sgu) * jnp.einsum('bse,ed->bsd', sgu, w_sgu_out[i]))
        x = x + jnp.einsum('bsd,de->bse', merged, w_out[i])

        ple_gate = jax.nn.sigmoid(jnp.einsum('bsd,de->bse', rms_norm(x, g_ple[i]), w_ple_gate[i]))
        x = x + ple_gate * jnp.einsum('bsp,pd->bsd', p[i], w_ple_proj[i])
    return rms_norm(x, g_final)
```

```python
import numpy as np
from contextlib import ExitStack

import concourse.bass as bass
import concourse.mybir as mybir
from concourse.bass_utils import run_bass_kernel_spmd

F32 = mybir.dt.float32
BF16 = mybir.dt.bfloat16
AF = mybir.ActivationFunctionType
ALU = mybir.AluOpType

N_CORES = 8
D = 1024
SEQ = 4096
TOK = 2 * SEQ
ST = 512
NST = TOK // ST
NCH = ST // 128
PLE = 256
NBLK = 24
NORM_EPS = 1e-6
GN_EPS = 1e-5

OFF_Q, OFF_K, OFF_V, OFF_RG, OFF_SU, OFF_SV, OFF_SG, OFF_MR, OFF_MS = 0, 512, 1024, 2048, 3072, 4096, 5120, 6144, 7168
B_Q, B_K, B_SV, B_SU, B_SG, B_V, B_RG = 0, 1, 2, 4, 6, 8, 10
B_MR0, B_MS0, B_MR1, B_MS1, B_RO1, B_SO1, B_RO0, B_SO0 = 12, 13, 14, 15, 16, 17, 18, 19
B_WO, B_PG = 20, 22

C_DT, C_XI, C_ZF, C_GF, C_GMIX, C_GPLE, C_END = 0, 512, 1024, 1536, 2560, 2568, 2576
W_MASK, W_WST, W_END = 0, 128, 640


class Reg:
    __slots__ = ("name", "w", "rs", "const", "strict")

    def __init__(self, name, const=False, strict=False):
        self.name = name
        self.w = None
        self.rs = []
        self.const = const
        self.strict = strict


class Stream:
    def __init__(self, sem):
        self.sem = sem
        self.count = 0


class Op:
    __slots__ = ("eng", "fn", "deps", "signal", "stream", "ndma", "sig")

    def __init__(self, eng, fn, stream, ndma):
        self.eng = eng
        self.fn = fn
        self.deps = []
        self.signal = False
        self.stream = stream
        self.ndma = ndma
        self.sig = None


class Sched:
    ENGS = ("pe", "act", "dve", "pool", "sp")

    def __init__(self):
        self.ops = {e: [] for e in self.ENGS}

    def add(self, eng, fn, reads=(), writes=(), stream=None, ndma=1):
        op = Op(eng, fn, stream, ndma)
        deps = {}
        for r in reads:
            if r.w is not None:
                deps[r.w] = "RAW"
        for w in writes:
            if w.w is not None:
                deps.setdefault(w.w, "WAW")
            for rd in w.rs:
                deps.setdefault(rd, "WAR")
        deps.pop(op, None)
        unread = set()
        for w in writes:
            if w.strict and w.w is not None and not w.rs:
                unread.add(w.w)
        for d, kind in deps.items():
            if (d.stream is not None or stream is not None or d.eng != eng or kind in ("RAW", "WAR")
                    or (kind == "WAW" and d in unread)):
                op.deps.append(d)
                d.signal = True
        for r in reads:
            if not r.const:
                r.rs.append(op)
        for w in writes:
            w.w = op
            w.rs = []
        self.ops[eng].append(op)
        return op

    def assign(self, engsem):
        cnt = {e: 0 for e in self.ENGS}
        for e in self.ENGS:
            for op in self.ops[e]:
                if op.stream is not None:
                    op.stream.count += 16 * op.ndma
                    op.sig = (op.stream.sem, op.stream.count)
                elif op.signal:
                    cnt[e] += 1
                    op.sig = (engsem[e], cnt[e])

    def emit(self, eng, h, engsem):
        waited = {}
        for op in self.ops[eng]:
            need = {}
            for d in op.deps:
                sem, val = d.sig
                k = id(sem)
                if k not in need or need[k][1] < val:
                    need[k] = (sem, val)
            for k, (sem, val) in need.items():
                if waited.get(k, 0) < val:
                    h.wait_ge(sem, val)
                    waited[k] = val
            ins = op.fn(h)
            if op.stream is not None:
                if not isinstance(ins, (list, tuple)):
                    ins = [ins]
                assert len(ins) == op.ndma
                for i in ins:
                    i.then_inc(op.stream.sem, 16)
            elif op.signal:
                ins.then_inc(engsem[eng], 1)


class Pool:
    def __init__(self, t, n, name, mkstream=None):
        self.t = t
        self.n = n
        self.i = 0
        self.last = 0
        self.name = name
        self.mk = mkstream
        self.regs = [Reg(f"{name}{k}") for k in range(n)]
        self._sin = [None] * n
        self._sout = [None] * n

    def get(self):
        k = self.i
        self.i = (self.i + 1) % self.n
        self.last = k
        return self.t[:, k], self.regs[k]

    def sin(self):
        if self._sin[self.last] is None:
            self._sin[self.last] = self.mk(f"si_{self.name}{self.last}")
        return self._sin[self.last]

    def sout(self):
        if self._sout[self.last] is None:
            self._sout[self.last] = self.mk(f"so_{self.name}{self.last}")
        return self._sout[self.last]


def v3(ap, a, b):
    return ap.rearrange("p (a b) -> p a b", a=a, b=b)


def build_program(decay, nst=NST):
    nc = bass.Bass("TRN2", target_bir_lowering=False)
    S = Sched()

    x_d = nc.dram_tensor("x", [TOK, D], F32, kind="ExternalInput").ap()
    p_d = nc.dram_tensor("p", [TOK, PLE], F32, kind="ExternalInput").ap()
    wall_d = nc.dram_tensor("wall", [NBLK, 128, 8, 512], F32, kind="ExternalInput").ap()
    wpp_d = nc.dram_tensor("wpp", [128, 2, 1024], F32, kind="ExternalInput").ap()
    cpack_d = nc.dram_tensor("cpack", [128, C_END], F32, kind="ExternalInput").ap()
    wspack_d = nc.dram_tensor("wspack", [128, W_END], F32, kind="ExternalInput").ap()
    cstab_d = nc.dram_tensor("cstab", [8, 128, 3 * 4 * 64], F32, kind="ExternalInput").ap()
    ident_d = nc.dram_tensor("ident", [128, 128], F32, kind="ExternalInput").ap()
    bs2_d = nc.dram_tensor("bs2", [2, 512], F32, kind="ExternalInput").ap()
    out_d = nc.dram_tensor("out", [TOK, D], F32, kind="ExternalOutput").ap()
    wsc_d = nc.dram_tensor("wsc", [NBLK, 128, 8, 512], BF16, kind="Internal").ap()

    with ExitStack() as es:
        E = es.enter_context

        def sb(name, shape, dt):
            return E(nc.sbuf_tensor(name, shape, dt))

        def mkstream(name):
            return Stream(E(nc.semaphore(name)))

        cpk = sb("cpk", [128, C_END], F32)
        ident = sb("identb", [128, 128], BF16)
        wsT = sb("wsT", [128, 512], BF16)
        ones2 = sb("ones2", [2, 128], BF16)
        browr = sb("browr", [2, 4, 512], BF16)
        mh = sb("mh", [128, 8], F32)
        wpp = sb("wppb", [128, 2, 1024], BF16)
        ring_t = sb("ring", [128, 4, 8 * 512], BF16)
        NRING = 4
        Fp = Pool(sb("Fp", [128, 4, 1024], F32), 4, "Fp", mkstream)
        F2 = Pool(sb("F2", [128, 4, 512], F32), 4, "F2", mkstream)
        Xr = Pool(sb("Xr", [128, 4, 1024], F32), 4, "Xr", mkstream)
        Hp = Pool(sb("Hp", [128, 5, 1024], BF16), 5, "Hp", mkstream)
        Bp = Pool(sb("Bp", [128, 2, 512], BF16), 2, "Bp", mkstream)
        sigl = sb("sigl", [128, 8, 512], BF16)
        r_sigl = [Reg(f"sigl{i}") for i in range(8)]
        junk = sb("junk", [128, 1024], BF16)
        hT = sb("hT", [128, 2, 8 * 512], BF16)
        vbuf = sb("vbuf", [128, 4, 1024], BF16)
        gu = sb("gu", [128, 8, 512], BF16)
        qkT = sb("qkT", [128, 8, 512], BF16)
        qxT = sb("qxT", [128, 4, 512], BF16)
        kz = sb("kz", [128, 4, 512], BF16)
        rg = sb("rg", [128, 4, 1024], BF16)
        state32 = sb("state32", [128, 1024], F32)
        state_bf = sb("state_bf", [128, 1024], BF16)
        retT = sb("retT", [128, 8, 512], BF16)
        mergedT = sb("mergedT", [128, 8, 512], BF16)
        H1T = Pool(sb("h1T", [128, 3, 1024], BF16), 3, "h1T")
        Pb = Pool(sb("pb", [128, 3, 256], BF16), 3, "pb", mkstream)
        PT = Pool(sb("pT", [128, 3, 256], BF16), 3, "pT")
        CS = Pool(sb("cs", [128, 2, 768], F32), 2, "cs", mkstream)
        Stt = Pool(sb("stat", [128, 28, 8], F32), 28, "stat")
        sv12 = sb("sv12", [128, 4, 12], F32)
        r_sv12 = [Reg(f"sv12_{c}") for c in range(NCH)]
        rt24 = sb("rt24", [128, 2, 24], F32)
        r_rt24 = [Reg("rt24_0"), Reg("rt24_1")]
        banks_t = [E(nc.psum_tensor(f"bank{i}", [128, 512], F32)) for i in range(8)]
        bank_regs = [Reg(f"bank{i}") for i in range(8)]
        bank_i = [0]

        ret_mode = [False]

        def bank():
            if ret_mode[0]:
                k = 5 + bank_i[0] % 3
            else:
                k = bank_i[0] % 8
            bank_i[0] += 1
            return banks_t[k], bank_regs[k]

        def bank_fixed(k):
            return banks_t[k], bank_regs[k]

        r_cpk = Reg("cpk", const=True)
        r_ident = Reg("ident", const=True)
        r_wsT = Reg("wsT", const=True)
        r_brow = Reg("brow", const=True)
        r_ones2 = Reg("ones2", const=True)
        r_mh = Reg("mh", const=True)
        r_wpp = Reg("wpp", const=True)
        r_junk = Reg("junk", strict=True)
        r_ring = [Reg(f"ring{i}") for i in range(NRING)]
        r_wsc = [[Reg(f"wsc{b}_{q}", const=True) for q in range(4)] for b in range(NBLK)]
        r_hT = [[Reg(f"hT{b}_{c}") for c in range(NCH)] for b in range(2)]
        r_vbuf = [Reg(f"vbuf{c}") for c in range(NCH)]
        r_gu = [Reg(f"gu{d}") for d in range(8)]
        r_qkT = [Reg(f"qkT{c}") for c in range(NCH)]
        r_qxT = [Reg(f"qxT{c}") for c in range(NCH)]
        r_kz = [Reg(f"kz{c}") for c in range(NCH)]
        r_rg = [Reg(f"rg{c}") for c in range(NCH)]
        r_s32 = [Reg(f"state32_{h}") for h in range(4)]
        r_sbf = Reg("state_bf")
        r_retT = [Reg(f"retT{c}") for c in range(NCH)]
        r_mT = [Reg(f"mT{d}") for d in range(8)]

        engsem = {e: E(nc.semaphore(f"sem_{e}")) for e in Sched.ENGS}
        s_ring = [mkstream(f"s_ring{i}") for i in range(NRING)]
        s_ringc = [mkstream(f"s_ringc{i}") for i in range(NRING)]

        cp3 = lambda a, b, n, w: v3(cpk[:, a:b], n, w)
        DTm = cpk[:, C_DT:C_XI]
        XIm = cp3(C_XI, C_ZF, 4, 128)
        ZFm = cpk[:, C_ZF:C_GF]
        GFm = cpk[:, C_GF:C_GMIX]
        GMIX = cpk[:, C_GMIX:C_GPLE]
        GPLE = cpk[:, C_GPLE:C_END]

        bs2_, r_b0 = F2.get()
        hi2f_, r_b2 = F2.get()
        lo2f_, r_b3 = F2.get()
        hi2_, r_b1 = Bp.get()
        bs2, hi2f, lo2f, hi2 = bs2_[0:2, :], hi2f_[0:2, :], lo2f_[0:2, :], hi2_[0:2, :]
        r_bs = [r_b0, r_b1, r_b2, r_b3]
        S.add("sp", lambda h: h.dma_start(out=bs2, in_=bs2_d[:, :]), writes=[r_bs[0]], stream=mkstream("s_c1"))
        S.add("pool", lambda h: h.dma_start(out=ident[:], in_=ident_d[:, :]), writes=[r_ident], stream=mkstream("s_c2"))
        S.add("pool", lambda h: h.memset(ones2[:], 1.0), writes=[r_ones2])
        S.add("pool", lambda h: h.memset(mh[:], -0.5), writes=[r_mh])
        wsp, wspr = Fp.get()
        S.add("sp", lambda h: h.dma_start(out=wsp[:, 0:W_END], in_=wspack_d[:, :]), writes=[wspr], stream=Fp.sin())
        S.add("dve", lambda h: h.tensor_tensor(out=v3(wsT[:], 4, 128), in0=v3(wsp[:, W_WST:W_END], 4, 128),
                                               in1=wsp[:, W_MASK:W_WST].unsqueeze(1).to_broadcast([128, 4, 128]), op=ALU.mult),
              reads=[wspr], writes=[r_wsT])
        S.add("dve", lambda h: h.tensor_copy(out=hi2, in_=bs2), reads=[r_bs[0]], writes=[r_bs[1]])
        S.add("dve", lambda h: h.tensor_copy(out=hi2f, in_=hi2), reads=[r_bs[1]], writes=[r_bs[2]])
        S.add("dve", lambda h: h.tensor_tensor(out=lo2f, in0=bs2, in1=hi2f, op=ALU.subtract),
              reads=[r_bs[0], r_bs[2]], writes=[r_bs[3]])
        br4 = browr[:].rearrange("p g (r i) -> p g r i", r=4, i=128)
        S.add("dve", lambda h: h.tensor_copy(out=br4, in_=v3(lo2f, 4, 128).unsqueeze(2).to_broadcast([2, 4, 4, 128])),
              reads=[r_bs[3]], writes=[r_brow])
        S.add("dve", lambda h: h.tensor_copy(out=br4[0:1], in_=v3(hi2_[0:1, :], 4, 128).unsqueeze(2).to_broadcast([1, 4, 4, 128])),
              reads=[r_bs[1], r_brow], writes=[r_brow])

        ring_live = {}
        load_seq = [0]
        total_loads = nst * NBLK

        def issue_next():
            n = load_seq[0]
            if n >= total_loads:
                return
            load_seq[0] = n + 1
            b = n % NBLK
            k = n % NRING
            if n < NBLK:
                S.add("pool", lambda h, b=b, k=k: h.dma_start(out=v3(ring_t[:, k], 8, 512), in_=wall_d[b, :, :, :]),
                      writes=[r_ring[k]], stream=s_ringc[k])
                S.add("sp", lambda h, b=b, k=k: h.dma_start(out=wsc_d[b, :, :, :], in_=v3(ring_t[:, k], 8, 512)),
                      reads=[r_ring[k]], writes=r_wsc[b], stream=mkstream(f"s_cv{b}"))
            else:
                S.add("sp", lambda h, b=b, k=k: h.dma_start(out=v3(ring_t[:, k], 8, 512), in_=wsc_d[b, :, :, :]),
                      reads=r_wsc[b], writes=[r_ring[k]], stream=s_ring[k])
            ring_live[(n // NBLK, b)] = k

        def blk(s, b):
            k = ring_live[(s, b)]
            return v3(ring_t[:, k], 8, 512), r_ring[k]

        def release():
            issue_next()

        def mm_group(out_ap, pairs):
            def fn(h):
                n = len(pairs)
                ins = None
                for i, (l, r) in enumerate(pairs):
                    ins = h.matmul(out_ap, lhsT=l, rhs=r, start=(i == 0), stop=(i == n - 1))
                return ins
            return fn

        def rms_rstd(ssq_ap, ssq_reg, n, eps):
            st, str_ = Stt.get()
            S.add("pool", lambda h: h.tensor_scalar(out=st[:, 0:1], in0=ssq_ap, scalar1=1.0 / n, scalar2=eps,
                                                    op0=ALU.mult, op1=ALU.add), reads=[ssq_reg], writes=[str_])
            S.add("pool", lambda h: h.tensor_tensor(out=st[:, 1:2], in0=st[:, 0:1], in1=mh[:, 0:1], op=ALU.pow),
                  reads=[str_, r_mh], writes=[str_])
            return st[:, 1:2], str_

        def transposes(src_ap, src_reg, nblk):
            bk, bkr = bank()
            pb = bk[:].bitcast(BF16)

            def fn(h):
                ins = None
                for k in range(nblk):
                    ins = h.transpose(out=pb[:, k * 128:(k + 1) * 128], in_=src_ap[:, k * 128:(k + 1) * 128],
                                      identity=ident[:])
                return ins
            S.add("pe", fn, reads=[src_reg, r_ident], writes=[bkr])
            return v3(pb[:, 0:nblk * 128], nblk, 128), bkr

        def hTv(s):
            return v3(hT[:, s % 2], 8, 512)

        keep = {}

        def A_ld(s, c):
            tok0 = s * ST
            if c == 0:
                csb, csr = CS.get()
                sp8 = s % 8
                S.add("sp", lambda h: h.dma_start(out=csb, in_=cstab_d[sp8, :, :]), writes=[csr], stream=CS.sin())
                keep[("cs", s)] = (csb.rearrange("p (t c f) -> p t c f", t=3, c=4, f=64), csr)
            xa, xar = Fp.get()
            S.add("sp", lambda h: h.dma_start(out=xa, in_=x_d[tok0 + c * 128: tok0 + (c + 1) * 128, :]),
                  writes=[xar], stream=Fp.sin())
            keep[("xa", s, c)] = (xa, xar)

        def A_sq(s, c):
            xa, xar = keep.pop(("xa", s, c))
            st, str_ = Stt.get()
            S.add("act", lambda h: h.activation(out=junk[:], in_=xa, func=AF.Square, accum_out=st[:, 0:1]),
                  reads=[xar], writes=[str_, r_junk])
            rstd, rstdr = rms_rstd(st[:, 0:1], str_, D, NORM_EPS)
            keep[("a", s, c)] = (xa, xar, rstd, rstdr)

        def A_cp(s, c):
            xa, xar, rstd, rstdr = keep.pop(("a", s, c))
            hb_, hbr = Hp.get()
            S.add("act", lambda h: h.activation(out=hb_, in_=xa, func=AF.Copy, scale=rstd),
                  reads=[xar, rstdr], writes=[hbr])
            keep[("h", s, c)] = (hb_, hbr)

        def A_tr(s, c):
            hb_, hbr = keep.pop(("h", s, c))
            tp, tpr = transposes(hb_, hbr, 8)
            hv = hTv(s)
            S.add("dve", lambda h: h.tensor_tensor(out=hv[:, :, c * 128:(c + 1) * 128], in0=tp,
                                                   in1=GMIX.unsqueeze(2).to_broadcast([128, 8, 128]), op=ALU.mult),
                  reads=[tpr, r_cpk], writes=[r_hT[s % 2][c]])

        def QK_proj(s, c):
            hv = hTv(s)
            cs4, csr = keep[("cs", s)]
            ta, tar = Fp.get()
            tb, tbr = Fp.get()
            tav = ta.rearrange("p (h t f) -> p h t f", h=8, t=2, f=64)
            tbv = tb.rearrange("p (h t f) -> p h t f", h=8, t=2, f=64)
            cosb = cs4[:, 0, c, :].unsqueeze(1).unsqueeze(1).to_broadcast([128, 4, 2, 64])
            nsin = cs4[:, 2, c, :].unsqueeze(1).to_broadcast([128, 4, 64])
            psin = cs4[:, 1, c, :].unsqueeze(1).to_broadcast([128, 4, 64])
            for j, bb in enumerate((B_Q, B_K)):
                w, wr = blk(s, bb)
                bk, bkr = bank()
                S.add("pe", mm_group(bk[:], [(hv[:, kt, c * 128:(c + 1) * 128], w[:, kt, :]) for kt in range(8)]),
                      reads=[r_hT[s % 2][c], wr], writes=[bkr])
                xv = bk[:].rearrange("p (h t f) -> p h t f", h=4, t=2, f=64)
                hs = slice(4 * j, 4 * j + 4)
                S.add("dve", lambda h, xv=xv, hs=hs: h.tensor_tensor(out=tav[:, hs], in0=xv, in1=cosb, op=ALU.mult),
                      reads=[bkr, csr], writes=[tar])
                S.add("dve", lambda h, xv=xv, hs=hs: h.tensor_tensor(out=tbv[:, hs, 0, :], in0=xv[:, :, 1, :], in1=nsin, op=ALU.mult),
                      reads=[bkr, csr], writes=[tbr])
                S.add("dve", lambda h, xv=xv, hs=hs: h.tensor_tensor(out=tbv[:, hs, 1, :], in0=xv[:, :, 0, :], in1=psin, op=ALU.mult),
                      reads=[bkr, csr], writes=[tbr])
            rot, rotr = Hp.get()
            S.add("dve", lambda h: h.tensor_tensor(out=rot, in0=ta, in1=tb, op=ALU.add), reads=[tar, tbr], writes=[rotr])
            S.add("pool", lambda h: h.tensor_tensor(out=kz[:, c, :], in0=rot[:, 512:1024], in1=ZFm, op=ALU.mult),
                  reads=[rotr, r_cpk], writes=[r_kz[c]])
            keep[("rot", s, c)] = (rot, rotr)

        def QK_tr(s, c):
            rot, rotr = keep.pop(("rot", s, c))
            tp, tpr = transposes(rot, rotr, 8)
            S.add("dve", lambda h: h.tensor_copy(out=qkT[:, :, c * 128:(c + 1) * 128], in_=tp), reads=[tpr], writes=[r_qkT[c]])
            S.add("dve", lambda h: h.tensor_tensor(out=qxT[:, :, c * 128:(c + 1) * 128], in0=tp[:, 0:4, :], in1=XIm, op=ALU.mult),
                  reads=[tpr, r_cpk], writes=[r_qxT[c]])

        def SV(s, c):
            hv = hTv(s)
            y, yr = Fp.get()
            mv, mvr = Stt.get()
            for hf in range(2):
                w, wr = blk(s, B_SV + hf)
                bk, bkr = bank()
                S.add("pe", mm_group(bk[:], [(hv[:, kt, c * 128:(c + 1) * 128], w[:, kt, :]) for kt in range(8)]),
                      reads=[r_hT[s % 2][c], wr], writes=[bkr])
                S.add("act", lambda h, bk=bk, hf=hf: h.activation(out=y[:, hf * 512:(hf + 1) * 512], in_=bk[:], func=AF.Gelu),
                      reads=[bkr], writes=[yr])
                S.add("dve", lambda h, hf=hf: h.bn_stats(out=sv12[:, c, hf * 6:(hf + 1) * 6], in_=y[:, hf * 512:(hf + 1) * 512]),
                      reads=[yr], writes=[r_sv12[c]])
            S.add("dve", lambda h: h.bn_aggr(out=mv[:, 0:2], in_=sv12[:, c, :].rearrange("p (a b) -> p a b", a=2, b=6)),
                  reads=[r_sv12[c]], writes=[mvr])
            S.add("dve", lambda h: h.tensor_scalar(out=mv[:, 2:3], in0=mv[:, 1:2], scalar1=GN_EPS, scalar2=None, op0=ALU.add),
                  reads=[mvr], writes=[mvr])
            S.add("pool", lambda h: h.tensor_tensor(out=mv[:, 3:4], in0=mv[:, 2:3], in1=mh[:, 0:1], op=ALU.pow), reads=[mvr, r_mh], writes=[mvr])
            S.add("dve", lambda h: h.tensor_scalar(out=vbuf[:, c, :], in0=y, scalar1=mv[:, 0:1], scalar2=mv[:, 3:4],
                                                   op0=ALU.subtract, op1=ALU.mult), reads=[yr, mvr], writes=[r_vbuf[c]])

        def SU(s, dt):
            hv = hTv(s)
            w, wr = blk(s, B_SU + dt // 4)
            m = dt % 4
            bk, bkr = bank()
            S.add("pe", mm_group(bk[:], [(w[:, kt, m * 128:(m + 1) * 128], hv[:, kt, :]) for kt in range(8)]),
                  reads=r_hT[s % 2] + [wr], writes=[bkr])
            S.add("act", lambda h: h.activation(out=gu[:, dt, :], in_=bk[:], func=AF.Gelu), reads=[bkr], writes=[r_gu[dt]])

        def SG(s, dt):
            hv = hTv(s)
            w, wr = blk(s, B_SG + dt // 4)
            m = dt % 4
            bk, bkr = bank()
            S.add("pe", mm_group(bk[:], [(w[:, kt, m * 128:(m + 1) * 128], hv[:, kt, :]) for kt in range(8)]),
                  reads=r_hT[s % 2] + [wr], writes=[bkr])
            sl, slr = Bp.get()
            S.add("act", lambda h: h.activation(out=sl, in_=bk[:], func=AF.Silu), reads=[bkr], writes=[slr])
            S.add("dve", lambda h: h.tensor_tensor(out=gu[:, dt, :], in0=gu[:, dt, :], in1=sl, op=ALU.mult),
                  reads=[slr, r_gu[dt]], writes=[r_gu[dt]])

        def MIX(s, dt):
            g = dt // 2
            bk, bkr = bank()

            def fn(h):
                h.matmul(bk[:], lhsT=ones2[0:2, :], rhs=browr[0:2, g, :], start=True, stop=False)
                ins = None
                for c in range(NCH):
                    ins = h.matmul(bk[:, c * 128:(c + 1) * 128], lhsT=vbuf[:, c, dt * 128:(dt + 1) * 128],
                                   rhs=wsT[:, g * 128:(g + 1) * 128], start=False, stop=(c == NCH - 1))
                return ins
            S.add("pe", fn, reads=r_vbuf + [r_wsT, r_ones2, r_brow], writes=[bkr])
            S.add("dve", lambda h: h.tensor_tensor(out=gu[:, dt, :], in0=bk[:], in1=gu[:, dt, :], op=ALU.mult),
                  reads=[bkr, r_gu[dt]], writes=[r_gu[dt]])

        def VRG(s, c, which):
            hv = hTv(s)
            for hf in range(2):
                w, wr = blk(s, (B_V if which == 0 else B_RG) + hf)
                bk, bkr = bank()
                S.add("pe", mm_group(bk[:], [(hv[:, kt, c * 128:(c + 1) * 128], w[:, kt, :]) for kt in range(8)]),
                      reads=[r_hT[s % 2][c], wr], writes=[bkr])
                if which == 0:
                    S.add("act", lambda h, bk=bk, hf=hf: h.activation(out=vbuf[:, c, hf * 512:(hf + 1) * 512], in_=bk[:], func=AF.Copy),
                          reads=[bkr], writes=[r_vbuf[c]])
                else:
                    S.add("act", lambda h, bk=bk, hf=hf: h.activation(out=rg[:, c, hf * 512:(hf + 1) * 512], in_=bk[:], func=AF.Silu),
                          reads=[bkr], writes=[r_rg[c]])

        def RET_s(s, c):
            cs_ = slice(c * 128, (c + 1) * 128)
            bS, bSr = bank_fixed(0)

            def fnS(h):
                ins = None
                for hd in range(4):
                    ins = h.matmul(bS[:, hd * 128:(hd + 1) * 128], lhsT=qkT[:, 4 + hd, cs_], rhs=qkT[:, hd, cs_], start=True, stop=True)
                return ins
            S.add("pe", fnS, reads=[r_qkT[c]], writes=[bSr])
            sdt, sdtr = Bp.get()
            S.add("dve", lambda h: h.tensor_tensor(out=sdt, in0=bS[:], in1=DTm, op=ALU.mult), reads=[bSr, r_cpk], writes=[sdtr])
            keep[("sdt", s, c)] = (sdt, sdtr)

        def RET_a(s, c):
            if s % 8 == 0 and c == 0:
                S.add("pool", lambda h: h.memset(state32[:], 0.0), writes=r_s32)
                S.add("pool", lambda h: h.memset(state_bf[:], 0.0), writes=[r_sbf])
            cs_ = slice(c * 128, (c + 1) * 128)
            sdt, sdtr = keep.pop(("sdt", s, c))
            bR = [bank_fixed(1), bank_fixed(2)] if c % 2 == 0 else [bank_fixed(3), bank_fixed(4)]

            def fnR(h):
                ins = None
                for hd in range(4):
                    o = bR[hd // 2][0][:, (hd % 2) * 256:(hd % 2 + 1) * 256]
                    h.matmul(o, lhsT=sdt[:, hd * 128:(hd + 1) * 128], rhs=vbuf[:, c, hd * 256:(hd + 1) * 256], start=True, stop=False)
                    ins = h.matmul(o, lhsT=qxT[:, hd, cs_], rhs=state_bf[:, hd * 256:(hd + 1) * 256], start=False, stop=True)
                return ins
            S.add("pe", fnR, reads=[sdtr, r_vbuf[c], r_qxT[c], r_sbf], writes=[bR[0][1], bR[1][1]])
            bK = [bank(), bank()]

            def fnK(h):
                ins = None
                for hd in range(4):
                    o = bK[hd // 2][0][:, (hd % 2) * 256:(hd % 2 + 1) * 256]
                    ins = h.matmul(o, lhsT=kz[:, c, hd * 128:(hd + 1) * 128], rhs=vbuf[:, c, hd * 256:(hd + 1) * 256], start=True, stop=True)
                return ins
            S.add("pe", fnK, reads=[r_kz[c], r_vbuf[c]], writes=[bK[0][1], bK[1][1]])
            for hd in range(4):
                S.add("dve", lambda h, hd=hd: h.scalar_tensor_tensor(
                    out=state32[:, hd * 256:(hd + 1) * 256], in0=state32[:, hd * 256:(hd + 1) * 256], scalar=float(decay[hd]),
                    in1=bK[hd // 2][0][:, (hd % 2) * 256:(hd % 2 + 1) * 256], op0=ALU.mult, op1=ALU.add),
                    reads=[r_s32[hd], bK[hd // 2][1]], writes=[r_s32[hd]])
            S.add("dve", lambda h: h.tensor_copy(out=state_bf[:], in_=state32[:]), reads=r_s32, writes=[r_sbf])
            mv, mvr = Stt.get()
            rs, rsr = Stt.get()
            par = c % 2
            for hd in range(4):
                src = bR[hd // 2][0][:, (hd % 2) * 256:(hd % 2 + 1) * 256]
                S.add("dve", lambda h, src=src, hd=hd: h.bn_stats(out=rt24[:, par, hd * 6:(hd + 1) * 6], in_=src),
                      reads=[bR[hd // 2][1]], writes=[r_rt24[par]])
            for hd in range(4):
                S.add("dve", lambda h, hd=hd: h.bn_aggr(out=mv[:, 2 * hd:2 * hd + 2], in_=rt24[:, par, hd * 6:(hd + 1) * 6].unsqueeze(1)),
                      reads=[r_rt24[par]], writes=[mvr])
            mv2 = mv.rearrange("p (h t) -> p h t", h=4, t=2)
            S.add("dve", lambda h: h.tensor_scalar(out=rs[:, 0:4], in0=mv2[:, :, 1], scalar1=GN_EPS, scalar2=None, op0=ALU.add),
                  reads=[mvr], writes=[rsr])
            S.add("pool", lambda h: h.tensor_tensor(out=rs[:, 4:8], in0=rs[:, 0:4], in1=mh[:, 0:4], op=ALU.pow), reads=[rsr, r_mh], writes=[rsr])
            rgs, rgsr = Hp.get()
            for hd in range(4):
                S.add("dve", lambda h, hd=hd: h.tensor_scalar(out=rgs[:, hd * 256:(hd + 1) * 256], in0=rg[:, c, hd * 256:(hd + 1) * 256],
                                                             scalar1=rs[:, 4 + hd:5 + hd], scalar2=None, op0=ALU.mult),
                      reads=[r_rg[c], rsr], writes=[rgsr])
            rn, rnr = Hp.get()
            for hd in range(4):
                src = bR[hd // 2][0][:, (hd % 2) * 256:(hd % 2 + 1) * 256]
                S.add("dve", lambda h, src=src, hd=hd: h.scalar_tensor_tensor(
                    out=rn[:, hd * 256:(hd + 1) * 256], in0=src, scalar=mv[:, 2 * hd:2 * hd + 1], in1=rgs[:, hd * 256:(hd + 1) * 256],
                    op0=ALU.subtract, op1=ALU.mult), reads=[bR[hd // 2][1], mvr, rgsr], writes=[rnr])
            keep[("rn", s, c)] = (rn, rnr)

        def RET_b(s, c):
            rn, rnr = keep.pop(("rn", s, c))
            tp, tpr = transposes(rn, rnr, 8)
            S.add("act", lambda h: h.activation(out=retT[:, :, c * 128:(c + 1) * 128], in_=tp, func=AF.Copy), reads=[tpr], writes=[r_retT[c]])

        def MG(s, which, dt):
            hv = hTv(s)
            bb = (B_MR0, B_MS0, B_MR1, B_MS1)[which + 2 * (dt // 4)]
            w, wr = blk(s, bb)
            m = dt % 4
            bk, bkr = bank()
            S.add("pe", mm_group(bk[:], [(w[:, kt, m * 128:(m + 1) * 128], hv[:, kt, :]) for kt in range(8)]),
                  reads=r_hT[s % 2] + [wr], writes=[bkr])
            if dt < 4:
                sg, sgr = sigl[:, which * 4 + dt, :], r_sigl[which * 4 + dt]
            else:
                slot = (dt - 4) if which == 0 else dt
                sg, sgr = mergedT[:, slot, :], r_mT[slot]
            S.add("act", lambda h: h.activation(out=sg, in_=bk[:], func=AF.Sigmoid), reads=[bkr], writes=[sgr])
            keep[("sig", s, which, dt)] = (sg, sgr)

        def OUTP_S(s, dt):
            m = dt % 4
            wso, wsor = blk(s, B_SO1 if dt >= 4 else B_SO0)
            sb_, sbr = keep.pop(("sig", s, 1, dt))
            bk2, bk2r = bank()
            S.add("pe", mm_group(bk2[:], [(wso[:, kt, m * 128:(m + 1) * 128], gu[:, kt, :]) for kt in range(8)]),
                  reads=r_gu + [wsor], writes=[bk2r])
            b_, br = F2.get()
            S.add("dve", lambda h: h.tensor_tensor(out=b_, in0=bk2[:], in1=sb_, op=ALU.mult), reads=[bk2r, sbr], writes=[br])
            keep[("b", s, dt)] = (b_, br)

        def OUTP_R(s, dt):
            m = dt % 4
            wro, wror = blk(s, B_RO1 if dt >= 4 else B_RO0)
            sa, sar = keep.pop(("sig", s, 0, dt))
            b_, br = keep.pop(("b", s, dt))
            bk1, bk1r = bank()
            S.add("pe", mm_group(bk1[:], [(wro[:, kt, m * 128:(m + 1) * 128], retT[:, kt, :]) for kt in range(8)]),
                  reads=r_retT + [wror], writes=[bk1r])
            a_, ar = F2.get()
            S.add("dve", lambda h: h.tensor_tensor(out=a_, in0=bk1[:], in1=sa, op=ALU.mult), reads=[bk1r, sar], writes=[ar])
            S.add("dve", lambda h: h.tensor_tensor(out=mergedT[:, dt, :], in0=a_, in1=b_, op=ALU.add), reads=[ar, br], writes=[r_mT[dt]])

        def Dst_loadx(s, c):
            t0 = s * ST + c * 128
            pool_ = Xr
            xr, xrr = pool_.get()
            xr_out = pool_.sout()
            S.add("sp", lambda h: h.dma_start(out=xr, in_=x_d[t0:t0 + 128, :]), writes=[xrr], stream=pool_.sin())
            keep[("dx", s, c)] = (xr, xrr, xr_out)

        def Dst_loadp(s, c):
            t0 = s * ST + c * 128
            pb_, pbr = Pb.get()
            S.add("pool", lambda h: h.dma_start(out=pb_, in_=p_d[t0:t0 + 128, :]), writes=[pbr], stream=Pb.sin())
            keep[("dp", s, c)] = (pb_, pbr)

        def Dst(s, c):
            cs_ = slice(c * 128, (c + 1) * 128)
            xr, xrr, xr_out = keep.pop(("dx", s, c))
            pb_, pbr = keep.pop(("dp", s, c))
            x1b, x1br = Hp.get()
            for hf in range(2):
                w, wr = blk(s, B_WO + hf)
                bk, bkr = bank()
                hs = slice(hf * 512, (hf + 1) * 512)
                S.add("pe", mm_group(bk[:], [(mergedT[:, kt, cs_], w[:, kt, :]) for kt in range(8)]), reads=r_mT + [wr], writes=[bkr])
                S.add("dve", lambda h, bk=bk, hs=hs: h.tensor_tensor(out=x1b[:, hs], in0=bk[:], in1=xr[:, hs], op=ALU.add),
                      reads=[bkr, xrr], writes=[x1br])
                S.add("dve", lambda h, bk=bk, hs=hs: h.tensor_tensor(out=xr[:, hs], in0=bk[:], in1=xr[:, hs], op=ALU.add),
                      reads=[bkr, xrr], writes=[xrr])
            st, str_ = Stt.get()
            S.add("act", lambda h: h.activation(out=junk[:], in_=xr, func=AF.Square, accum_out=st[:, 0:1]), reads=[xrr], writes=[str_, r_junk])
            rstd, rstdr = rms_rstd(st[:, 0:1], str_, D, NORM_EPS)
            keep[("d", s, c)] = (xr, xrr, xr_out, pb_, pbr, x1b, x1br, rstd, rstdr)

        def D_tr(s, c):
            xr, xrr, xr_out, pb_, pbr, x1b, x1br, rstd, rstdr = keep.pop(("d", s, c))
            tp, tpr = transposes(x1b, x1br, 8)
            h1t, h1tr = H1T.get()
            h1t3 = v3(h1t, 8, 128)
            S.add("dve", lambda h: h.tensor_tensor(out=h1t3, in0=tp, in1=GPLE.unsqueeze(2).to_broadcast([128, 8, 128]), op=ALU.mult),
                  reads=[tpr, r_cpk], writes=[h1tr])
            if True:
                tpp, tppr = transposes(pb_, pbr, 2)
                pt, ptr = PT.get()
                pt3 = v3(pt, 2, 128)
                S.add("dve", lambda h: h.tensor_copy(out=pt3, in_=tpp), reads=[tppr], writes=[ptr])
                keep[("e", s, c)] = (xr, xrr, xr_out, h1t3, h1tr, pt3, ptr, None, None, rstd, rstdr)
            else:
                keep[("e", s, c)] = (xr, xrr, xr_out, h1t3, h1tr, None, None, pb_, pbr, rstd, rstdr)

        def Est(s, c):
            t0 = s * ST + c * 128
            xr, xrr, xr_out, h1t3, h1tr, pt3, ptr, pb_, pbr, rstd1, rstd1r = keep.pop(("e", s, c))
            if pt3 is None:
                tpp, tppr = transposes(pb_, pbr, 2)
                pt, ptr = PT.get()
                pt3 = v3(pt, 2, 128)
                S.add("dve", lambda h: h.tensor_copy(out=pt3, in_=tpp), reads=[tppr], writes=[ptr])
            for hf in range(2):
                w, wr = blk(s, B_PG + hf)
                bkg, bkgr = bank()
                S.add("pe", mm_group(bkg[:], [(h1t3[:, kt, :], w[:, kt, :]) for kt in range(8)]), reads=[h1tr, wr], writes=[bkgr])
                gt, gtr = F2.get()
                S.add("act", lambda h, gt=gt, bkg=bkg: h.activation(out=gt, in_=bkg[:], func=AF.Sigmoid, scale=rstd1),
                      reads=[bkgr, rstd1r], writes=[gtr])
                bkp, bkpr = bank()
                S.add("pe", mm_group(bkp[:], [(pt3[:, k2, :], wpp[:, k2, hf * 512:(hf + 1) * 512]) for k2 in range(2)]),
                      reads=[ptr, r_wpp], writes=[bkpr])
                S.add("dve", lambda h, gt=gt, bkp=bkp: h.tensor_tensor(out=gt, in0=bkp[:], in1=gt, op=ALU.mult), reads=[bkpr, gtr], writes=[gtr])
                S.add("pool", lambda h, gt=gt, hf=hf: h.tensor_tensor(out=xr[:, hf * 512:(hf + 1) * 512], in0=xr[:, hf * 512:(hf + 1) * 512], in1=gt, op=ALU.add),
                      reads=[xrr, gtr], writes=[xrr])
            keep[("t", s, c)] = (xr, xrr, xr_out)

        def Est_tail(s, c):
            t0 = s * ST + c * 128
            xr, xrr, xr_out = keep.pop(("t", s, c))
            st, str_ = Stt.get()
            S.add("act", lambda h: h.activation(out=junk[:], in_=xr, func=AF.Square, accum_out=st[:, 0:1]), reads=[xrr], writes=[str_, r_junk])
            rstd, rstdr = rms_rstd(st[:, 0:1], str_, D, NORM_EPS)
            S.add("dve", lambda h: h.scalar_tensor_tensor(out=xr, in0=xr, scalar=rstd, in1=GFm, op0=ALU.mult, op1=ALU.mult),
                  reads=[xrr, rstdr, r_cpk], writes=[xrr])
            S.add("pool", lambda h: h.dma_start(out=out_d[t0:t0 + 128, :], in_=xr), reads=[xrr], writes=[], stream=xr_out)

        for c in range(NCH):
            A_ld(0, c)
        S.add("sp", lambda h: h.dma_start(out=cpk[:], in_=cpack_d[:, :]), writes=[r_cpk], stream=mkstream("s_c0"))
        for _ in range(NRING):
            issue_next()
        S.add("pool", lambda h: h.dma_start(out=wpp[:], in_=wpp_d[:, :, :]), writes=[r_wpp], stream=mkstream("s_c3"))
        for c in range(NCH):
            A_sq(0, c)
        for c in range(NCH):
            A_cp(0, c)
        for c in range(NCH):
            A_tr(0, c)
        for s in range(nst):
            nxt = s + 1 < nst
            for c in range(NCH):
                QK_proj(s, c)
            release(); release()
            for c in range(NCH):
                SV(s, c)
            release(); release()
            for dt in range(8):
                SU(s, dt)
                if dt % 4 == 3:
                    release()
            for dt in range(8):
                SG(s, dt)
                if dt % 4 == 3:
                    release()
            for c in range(NCH):
                QK_tr(s, c)
            for dt in range(8):
                MIX(s, dt)
            for c in range(NCH):
                VRG(s, c, 0)
            release(); release()
            for c in range(NCH):
                VRG(s, c, 1)
            release(); release()
            mg_plan = [(0, 0), (1, 0), (0, 4), (1, 4)]
            if nxt:
                for c in range(NCH):
                    A_ld(s + 1, c)
            ret_mode[0] = True
            RET_s(s, 0)
            for c in range(NCH):
                if c + 1 < NCH:
                    RET_s(s, c + 1)
                RET_a(s, c)
                if c >= 1:
                    RET_b(s, c - 1)
                which, d0 = mg_plan[c]
                for m in range(4):
                    MG(s, which, d0 + m)
                release()
            ret_mode[0] = False
            for c in range(NCH):
                Dst_loadx(s, c)
            Dst_loadp(s, 0)
            Dst_loadp(s, 1)
            Dst_loadp(s, 2)
            OUTP_S(s, 4)
            OUTP_S(s, 5)
            RET_b(s, NCH - 1)
            if nxt:
                for c in range(NCH):
                    A_sq(s + 1, c)
            OUTP_R(s, 4)
            OUTP_R(s, 5)
            if nxt:
                for c in range(NCH):
                    A_cp(s + 1, c)
            for i, dt in enumerate((6, 7, 0, 1, 2, 3)):
                OUTP_S(s, dt)
                OUTP_R(s, dt)
                if dt in (7, 3):
                    release(); release()
                if nxt and i in (1, 3, 5):
                    A_tr(s + 1, i // 2)
            if nxt:
                A_tr(s + 1, 3)
            Dst(s, 0)
            Dst(s, 1)
            D_tr(s, 0)
            Dst_loadp(s, 3)
            Dst(s, 2)
            D_tr(s, 1)
            Dst(s, 3)
            release(); release()
            D_tr(s, 2)
            Est(s, 0)
            D_tr(s, 3)
            Est(s, 1)
            Est_tail(s, 0)
            Est(s, 2)
            Est_tail(s, 1)
            Est(s, 3)
            release(); release()
            Est_tail(s, 2)
            Est_tail(s, 3)

        S.assign(engsem)
        build_program.sbuf_left = nc.sbuf_bytes_remaining
        block = E(nc.Block())

        @block.sync
        def _(h):
            S.emit("sp", h, engsem)
            for so in Xr._sout + Fp._sout:
                if so is not None:
                    h.wait_ge(so.sem, so.count)

        @block.gpsimd
        def _(h):
            S.emit("pool", h, engsem)
            for so in Xr._sout + Fp._sout:
                if so is not None:
                    h.wait_ge(so.sem, so.count)

        @block.vector
        def _(h):
            S.emit("dve", h, engsem)

        @block.scalar
        def _(h):
            S.emit("act", h, engsem)

        @block.tensor
        def _(h):
            S.emit("pe", h, engsem)

    return nc


def _module_constants():
    f32, f64 = np.float32, np.float64
    hidx = np.arange(4, dtype=f64)
    log_g = np.log(1.0 - np.power(2.0, -5.0 - hidx))
    idx = np.arange(128, dtype=f64)
    diff = idx[:, None] - idx[None, :]
    scale = 128.0 ** -0.5
    dec = np.where(diff[None] >= 0, np.exp(np.maximum(diff, 0.0)[None] * log_g[:, None, None]), 0.0)
    DT = np.ascontiguousarray((dec * scale).transpose(2, 0, 1)).reshape(128, 512).astype(f32)
    zeta = np.exp((127.0 - idx)[:, None] * log_g[None, :])
    xi = np.exp((idx + 1.0)[:, None] * log_g[None, :]) * scale
    XI = np.broadcast_to(xi.T.reshape(1, 512), (128, 512)).astype(f32)
    ZF = np.repeat(zeta, 128, axis=1).astype(f32)
    decay = np.exp(128.0 * log_g).astype(f32)
    maskT = (idx[None, :] >= idx[:, None]).astype(f32)
    half = 64
    inv = np.power(10000.0, -(np.arange(half, dtype=f64) / half))
    ang = np.arange(SEQ, dtype=f64)[:, None] * inv[None, :]
    cos = np.cos(ang)
    sin = np.sin(ang)
    tab = np.stack([cos, sin, -sin], 0).reshape(3, 8, 4, 128, half)
    cstab = np.ascontiguousarray(tab.transpose(1, 3, 0, 2, 4)).reshape(8, 128, 3 * 4 * half).astype(f32)
    return DT, XI, ZF, decay, maskT, cstab


def _blockify(w, col0):
    return np.ascontiguousarray(w[:, col0:col0 + 512].reshape(8, 128, 512).transpose(1, 0, 2))


_CACHE = {}
_NST = NST


def kernel(x, p, w_in, w_ret_out, w_sgu_out, w_out, sgu_ws, sgu_bs, w_ple_gate, w_ple_proj, g_mixer, g_ple, g_final):
    f32 = np.float32
    x = np.asarray(x, dtype=f32)
    p = np.asarray(p, dtype=f32)
    w_in = np.asarray(w_in, dtype=f32)[0]
    w_ro = np.asarray(w_ret_out, dtype=f32)[0]
    w_so = np.asarray(w_sgu_out, dtype=f32)[0]
    w_o = np.asarray(w_out, dtype=f32)[0]
    w_pg = np.asarray(w_ple_gate, dtype=f32)[0]
    w_pp = np.asarray(w_ple_proj, dtype=f32)[0]
    ws = np.asarray(sgu_ws, dtype=f32)[0]
    bs = np.asarray(sgu_bs, dtype=f32)[0]
    gm = np.asarray(g_mixer, dtype=f32)[0]
    gp = np.asarray(g_ple, dtype=f32)[0]
    gf = np.asarray(g_final, dtype=f32)

    DT, XI, ZF, decay, maskT, cstab = _module_constants()

    wall = np.empty((NBLK, 128, 8, 512), dtype=f32)
    order = [(w_in, OFF_Q), (w_in, OFF_K), (w_in, OFF_SV), (w_in, OFF_SV + 512), (w_in, OFF_SU), (w_in, OFF_SU + 512),
             (w_in, OFF_SG), (w_in, OFF_SG + 512), (w_in, OFF_V), (w_in, OFF_V + 512), (w_in, OFF_RG), (w_in, OFF_RG + 512),
             (w_in, OFF_MR), (w_in, OFF_MS), (w_in, OFF_MR + 512), (w_in, OFF_MS + 512),
             (w_ro, 512), (w_so, 512), (w_ro, 0), (w_so, 0), (w_o, 0), (w_o, 512), (w_pg, 0), (w_pg, 512)]
    for i, (w, c0) in enumerate(order):
        wall[i] = _blockify(w, c0)
    wpp = np.ascontiguousarray(w_pp.reshape(2, 128, 1024).transpose(1, 0, 2))
    cpack = np.empty((128, C_END), dtype=f32)
    cpack[:, C_DT:C_XI] = DT
    cpack[:, C_XI:C_ZF] = XI
    cpack[:, C_ZF:C_GF] = ZF
    cpack[:, C_GF:C_GMIX] = np.broadcast_to(gf[None, :], (128, D))
    wspack = np.empty((128, W_END), dtype=f32)
    wspack[:, W_MASK:W_WST] = maskT
    wspack[:, W_WST:W_END] = ws.transpose(2, 0, 1).reshape(128, 512)
    cpack[:, C_GMIX:C_GPLE] = gm.reshape(8, 128).T
    cpack[:, C_GPLE:C_END] = gp.reshape(8, 128).T
    bs2 = np.ascontiguousarray(np.broadcast_to(bs.reshape(1, 512), (2, 512))).astype(f32)
    ident = np.eye(128, dtype=f32)

    key = ("prog", _NST)
    if key not in _CACHE:
        _CACHE[key] = build_program(decay, nst=_NST)
    nc = _CACHE[key]

    xs = x.reshape(N_CORES, TOK, D)
    ps = p.reshape(N_CORES, TOK, PLE)
    in_maps = []
    for i in range(N_CORES):
        in_maps.append({"x": xs[i], "p": ps[i], "wall": wall, "wpp": wpp, "cpack": cpack, "wspack": wspack, "cstab": cstab,
                        "ident": ident, "bs2": bs2})
    res = run_bass_kernel_spmd(nc, in_maps, core_ids=list(range(N_CORES)))
    out = np.stack([np.asarray(r["out"], dtype=f32) for r in res.results], 0)
    return out.reshape(16, SEQ, D)
```

```python
import numpy as np
from contextlib import ExitStack

import concourse.bass as bass
import concourse.mybir as mybir
from concourse.bass_utils import run_bass_kernel_spmd

F32 = mybir.dt.float32
BF16 = mybir.dt.bfloat16
AF = mybir.ActivationFunctionType
ALU = mybir.AluOpType

N_CORES = 8
D = 1024
SEQ = 4096
TOK = 2 * SEQ
ST = 512
NST = TOK // ST
NCH = ST // 128
PLE = 256
NBLK = 24
NORM_EPS = 1e-6
GN_EPS = 1e-5

OFF_Q, OFF_K, OFF_V, OFF_RG, OFF_SU, OFF_SV, OFF_SG, OFF_MR, OFF_MS = 0, 512, 1024, 2048, 3072, 4096, 5120, 6144, 7168
B_Q, B_K, B_SV, B_SU, B_SG, B_V, B_RG = 0, 1, 2, 4, 6, 8, 10
B_MR0, B_MS0, B_MR1, B_MS1, B_RO1, B_SO1, B_RO0, B_SO0 = 12, 13, 14, 15, 16, 17, 18, 19
B_WO, B_PG = 20, 22

C_DT, C_XI, C_ZF, C_GF, C_GMIX, C_GPLE, C_END = 0, 512, 1024, 1536, 2560, 2568, 2576
W_MASK, W_WST, W_END = 0, 128, 640


class Reg:
    __slots__ = ("name", "w", "rs", "const", "strict")

    def __init__(self, name, const=False, strict=False):
        self.name = name
        self.w = None
        self.rs = []
        self.const = const
        self.strict = strict


class Stream:
    def __init__(self, sem):
        self.sem = sem
        self.count = 0


class Op:
    __slots__ = ("eng", "fn", "deps", "signal", "stream", "ndma", "sig")

    def __init__(self, eng, fn, stream, ndma):
        self.eng = eng
        self.fn = fn
        self.deps = []
        self.signal = False
        self.stream = stream
        self.ndma = ndma
        self.sig = None


class Sched:
    ENGS = ("pe", "act", "dve", "pool", "sp")

    def __init__(self):
        self.ops = {e: [] for e in self.ENGS}

    def add(self, eng, fn, reads=(), writes=(), stream=None, ndma=1):
        op = Op(eng, fn, stream, ndma)
        deps = {}
        for r in reads:
            if r.w is not None:
                deps[r.w] = "RAW"
        for w in writes:
            if w.w is not None:
                deps.setdefault(w.w, "WAW")
            for rd in w.rs:
                deps.setdefault(rd, "WAR")
        deps.pop(op, None)
        unread = set()
        for w in writes:
            if w.strict and w.w is not None and not w.rs:
                unread.add(w.w)
        for d, kind in deps.items():
            if (d.stream is not None or stream is not None or d.eng != eng or kind in ("RAW", "WAR")
                    or (kind == "WAW" and d in unread)):
                op.deps.append(d)
                d.signal = True
        for r in reads:
            if not r.const:
                r.rs.append(op)
        for w in writes:
            w.w = op
            w.rs = []
        self.ops[eng].append(op)
        return op

    def assign(self, engsem):
        cnt = {e: 0 for e in self.ENGS}
        for e in self.ENGS:
            for op in self.ops[e]:
                if op.stream is not None:
                    op.stream.count += 16 * op.ndma
                    op.sig = (op.stream.sem, op.stream.count)
                elif op.signal:
                    cnt[e] += 1
                    op.sig = (engsem[e], cnt[e])

    def emit(self, eng, h, engsem):
        waited = {}
        for op in self.ops[eng]:
            need = {}
            for d in op.deps:
                sem, val = d.sig
                k = id(sem)
                if k not in need or need[k][1] < val:
                    need[k] = (sem, val)
            for k, (sem, val) in need.items():
                if waited.get(k, 0) < val:
                    h.wait_ge(sem, val)
                    waited[k] = val
            ins = op.fn(h)
            if op.stream is not None:
                if not isinstance(ins, (list, tuple)):
                    ins = [ins]
                assert len(ins) == op.ndma
                for i in ins:
                    i.then_inc(op.stream.sem, 16)
            elif op.signal:
                ins.then_inc(engsem[eng], 1)


class Pool:
    def __init__(self, t, n, name, mkstream=None):
        self.t = t
        self.n = n
        self.i = 0
        self.last = 0
        self.name = name
        self.mk = mkstream
        self.regs = [Reg(f"{name}{k}") for k in range(n)]
        self._sin = [None] * n
        self._sout = [None] * n

    def get(self):
        k = self.i
        self.i = (self.i + 1) % self.n
        self.last = k
        return self.t[:, k], self.regs[k]

    def sin(self):
        if self._sin[self.last] is None:
            self._sin[self.last] = self.mk(f"si_{self.name}{self.last}")
        return self._sin[self.last]

    def sout(self):
        if self._sout[self.last] is None:
            self._sout[self.last] = self.mk(f"so_{self.name}{self.last}")
        return self._sout[self.last]


def v3(ap, a, b):
    return ap.rearrange("p (a b) -> p a b", a=a, b=b)


def build_program(decay, nst=NST):
    nc = bass.Bass("TRN2", target_bir_lowering=False)
    S = Sched()

    x_d = nc.dram_tensor("x", [TOK, D], F32, kind="ExternalInput").ap()
    p_d = nc.dram_tensor("p", [TOK, PLE], F32, kind="ExternalInput").ap()
    wall_d = nc.dram_tensor("wall", [NBLK, 128, 8, 512], F32, kind="ExternalInput").ap()
    wpp_d = nc.dram_tensor("wpp", [128, 2, 1024], F32, kind="ExternalInput").ap()
    cpack_d = nc.dram_tensor("cpack", [128, C_END], F32, kind="ExternalInput").ap()
    wspack_d = nc.dram_tensor("wspack", [128, W_END], F32, kind="ExternalInput").ap()
    cstab_d = nc.dram_tensor("cstab", [8, 128, 3 * 4 * 64], F32, kind="ExternalInput").ap()
    ident_d = nc.dram_tensor("ident", [128, 128], F32, kind="ExternalInput").ap()
    bs2_d = nc.dram_tensor("bs2", [2, 512], F32, kind="ExternalInput").ap()
    out_d = nc.dram_tensor("out", [TOK, D], F32, kind="ExternalOutput").ap()
    wsc_d = nc.dram_tensor("wsc", [NBLK, 128, 8, 512], BF16, kind="Internal").ap()

    with ExitStack() as es:
        E = es.enter_context

        def sb(name, shape, dt):
            return E(nc.sbuf_tensor(name, shape, dt))

        def mkstream(name):
            return Stream(E(nc.semaphore(name)))

        cpk = sb("cpk", [128, C_END], F32)
        ident = sb("identb", [128, 128], BF16)
        wsT = sb("wsT", [128, 512], BF16)
        ones2 = sb("ones2", [2, 128], BF16)
        browr = sb("browr", [2, 4, 512], BF16)
        mh = sb("mh", [128, 8], F32)
        wpp = sb("wppb", [128, 2, 1024], BF16)
        ring_t = sb("ring", [128, 4, 8 * 512], BF16)
        NRING = 4
        Fp = Pool(sb("Fp", [128, 4, 1024], F32), 4, "Fp", mkstream)
        F2 = Pool(sb("F2", [128, 4, 512], F32), 4, "F2", mkstream)
        Xr = Pool(sb("Xr", [128, 4, 1024], F32), 4, "Xr", mkstream)
        Hp = Pool(sb("Hp", [128, 5, 1024], BF16), 5, "Hp", mkstream)
        Bp = Pool(sb("Bp", [128, 2, 512], BF16), 2, "Bp", mkstream)
        sigl = sb("sigl", [128, 8, 512], BF16)
        r_sigl = [Reg(f"sigl{i}") for i in range(8)]
        junk = sb("junk", [128, 1024], BF16)
        hT = sb("hT", [128, 2, 8 * 512], BF16)
        vbuf = sb("vbuf", [128, 4, 1024], BF16)
        gu = sb("gu", [128, 8, 512], BF16)
        qkT = sb("qkT", [128, 8, 512], BF16)
        qxT = sb("qxT", [128, 4, 512], BF16)
        kz = sb("kz", [128, 4, 512], BF16)
        rg = sb("rg", [128, 4, 1024], BF16)
        state32 = sb("state32", [128, 1024], F32)
        state_bf = sb("state_bf", [128, 1024], BF16)
        retT = sb("retT", [128, 8, 512], BF16)
        mergedT = sb("mergedT", [128, 8, 512], BF16)
        H1T = Pool(sb("h1T", [128, 3, 1024], BF16), 3, "h1T")
        Pb = Pool(sb("pb", [128, 3, 256], BF16), 3, "pb", mkstream)
        PT = Pool(sb("pT", [128, 3, 256], BF16), 3, "pT")
        CS = Pool(sb("cs", [128, 2, 768], F32), 2, "cs", mkstream)
        Stt = Pool(sb("stat", [128, 28, 8], F32), 28, "stat")
        sv12 = sb("sv12", [128, 4, 12], F32)
        r_sv12 = [Reg(f"sv12_{c}") for c in range(NCH)]
        rt24 = sb("rt24", [128, 2, 24], F32)
        r_rt24 = [Reg("rt24_0"), Reg("rt24_1")]
        banks_t = [E(nc.psum_tensor(f"bank{i}", [128, 512], F32)) for i in range(8)]
        bank_regs = [Reg(f"bank{i}") for i in range(8)]
        bank_i = [0]

        ret_mode = [False]

        def bank():
            if ret_mode[0]:
                k = (0, 5, 6, 7)[bank_i[0] % 4]
            else:
                k = bank_i[0] % 8
            bank_i[0] += 1
            return banks_t[k], bank_regs[k]

        def bank_fixed(k):
            return banks_t[k], bank_regs[k]

        r_cpk = Reg("cpk", const=True)
        r_ident = Reg("ident", const=True)
        r_wsT = Reg("wsT", const=True)
        r_brow = Reg("brow", const=True)
        r_ones2 = Reg("ones2", const=True)
        r_mh = Reg("mh", const=True)
        r_wpp = Reg("wpp", const=True)
        r_junk = Reg("junk", strict=True)
        r_ring = [Reg(f"ring{i}") for i in range(NRING)]
        r_wsc = [[Reg(f"wsc{b}_{q}", const=True) for q in range(4)] for b in range(NBLK)]
        r_hT = [[Reg(f"hT{b}_{c}") for c in range(NCH)] for b in range(2)]
        r_vbuf = [Reg(f"vbuf{c}") for c in range(NCH)]
        r_gu = [Reg(f"gu{d}") for d in range(8)]
        r_qkT = [Reg(f"qkT{c}") for c in range(NCH)]
        r_qxT = [Reg(f"qxT{c}") for c in range(NCH)]
        r_kz = [Reg(f"kz{c}") for c in range(NCH)]
        r_rg = [Reg(f"rg{c}") for c in range(NCH)]
        r_s32 = [Reg(f"state32_{h}") for h in range(4)]
        r_sbf = Reg("state_bf")
        r_retT = [Reg(f"retT{c}") for c in range(NCH)]
        r_mT = [Reg(f"mT{d}") for d in range(8)]

        engsem = {e: E(nc.semaphore(f"sem_{e}")) for e in Sched.ENGS}
        s_ring = [mkstream(f"s_ring{i}") for i in range(NRING)]
        s_ringc = [mkstream(f"s_ringc{i}") for i in range(NRING)]

        cp3 = lambda a, b, n, w: v3(cpk[:, a:b], n, w)
        DTm = cpk[:, C_DT:C_XI]
        XIm = cp3(C_XI, C_ZF, 4, 128)
        ZFm = cpk[:, C_ZF:C_GF]
        GFm = cpk[:, C_GF:C_GMIX]
        GMIX = cpk[:, C_GMIX:C_GPLE]
        GPLE = cpk[:, C_GPLE:C_END]

        bs2_, r_b0 = F2.get()
        hi2f_, r_b2 = F2.get()
        lo2f_, r_b3 = F2.get()
        hi2_, r_b1 = Bp.get()
        bs2, hi2f, lo2f, hi2 = bs2_[0:2, :], hi2f_[0:2, :], lo2f_[0:2, :], hi2_[0:2, :]
        r_bs = [r_b0, r_b1, r_b2, r_b3]
        S.add("sp", lambda h: h.dma_start(out=bs2, in_=bs2_d[:, :]), writes=[r_bs[0]], stream=mkstream("s_c1"))
        S.add("pool", lambda h: h.dma_start(out=ident[:], in_=ident_d[:, :]), writes=[r_ident], stream=mkstream("s_c2"))
        S.add("pool", lambda h: h.memset(ones2[:], 1.0), writes=[r_ones2])
        S.add("pool", lambda h: h.memset(mh[:], -0.5), writes=[r_mh])
        wsp, wspr = Fp.get()
        S.add("sp", lambda h: h.dma_start(out=wsp[:, 0:W_END], in_=wspack_d[:, :]), writes=[wspr], stream=Fp.sin())
        S.add("dve", lambda h: h.tensor_tensor(out=v3(wsT[:], 4, 128), in0=v3(wsp[:, W_WST:W_END], 4, 128),
                                               in1=wsp[:, W_MASK:W_WST].unsqueeze(1).to_broadcast([128, 4, 128]), op=ALU.mult),
              reads=[wspr], writes=[r_wsT])
        S.add("dve", lambda h: h.tensor_copy(out=hi2, in_=bs2), reads=[r_bs[0]], writes=[r_bs[1]])
        S.add("dve", lambda h: h.tensor_copy(out=hi2f, in_=hi2), reads=[r_bs[1]], writes=[r_bs[2]])
        S.add("dve", lambda h: h.tensor_tensor(out=lo2f, in0=bs2, in1=hi2f, op=ALU.subtract),
              reads=[r_bs[0], r_bs[2]], writes=[r_bs[3]])
        br4 = browr[:].rearrange("p g (r i) -> p g r i", r=4, i=128)
        S.add("dve", lambda h: h.tensor_copy(out=br4, in_=v3(lo2f, 4, 128).unsqueeze(2).to_broadcast([2, 4, 4, 128])),
              reads=[r_bs[3]], writes=[r_brow])
        S.add("dve", lambda h: h.tensor_copy(out=br4[0:1], in_=v3(hi2_[0:1, :], 4, 128).unsqueeze(2).to_broadcast([1, 4, 4, 128])),
              reads=[r_bs[1], r_brow], writes=[r_brow])

        ring_live = {}
        load_seq = [0]
        total_loads = nst * NBLK

        def issue_next():
            n = load_seq[0]
            if n >= total_loads:
                return
            load_seq[0] = n + 1
            b = n % NBLK
            k = n % NRING
            if n < NBLK:
                S.add("pool", lambda h, b=b, k=k: h.dma_start(out=v3(ring_t[:, k], 8, 512), in_=wall_d[b, :, :, :]),
                      writes=[r_ring[k]], stream=s_ringc[k])
                S.add("sp", lambda h, b=b, k=k: h.dma_start(out=wsc_d[b, :, :, :], in_=v3(ring_t[:, k], 8, 512)),
                      reads=[r_ring[k]], writes=r_wsc[b], stream=mkstream(f"s_cv{b}"))
            else:
                S.add("sp", lambda h, b=b, k=k: h.dma_start(out=v3(ring_t[:, k], 8, 512), in_=wsc_d[b, :, :, :]),
                      reads=r_wsc[b], writes=[r_ring[k]], stream=s_ring[k])
            ring_live[(n // NBLK, b)] = k

        def blk(s, b):
            k = ring_live[(s, b)]
            return v3(ring_t[:, k], 8, 512), r_ring[k]

        def release():
            issue_next()

        def mm_group(out_ap, pairs):
            def fn(h):
                n = len(pairs)
                ins = None
                for i, (l, r) in enumerate(pairs):
                    ins = h.matmul(out_ap, lhsT=l, rhs=r, start=(i == 0), stop=(i == n - 1))
                return ins
            return fn

        def rms_rstd(ssq_ap, ssq_reg, n, eps):
            st, str_ = Stt.get()
            S.add("pool", lambda h: h.tensor_scalar(out=st[:, 0:1], in0=ssq_ap, scalar1=1.0 / n, scalar2=eps,
                                                    op0=ALU.mult, op1=ALU.add), reads=[ssq_reg], writes=[str_])
            S.add("pool", lambda h: h.tensor_tensor(out=st[:, 1:2], in0=st[:, 0:1], in1=mh[:, 0:1], op=ALU.pow),
                  reads=[str_, r_mh], writes=[str_])
            return st[:, 1:2], str_

        def transposes(src_ap, src_reg, nblk):
            bk, bkr = bank()
            pb = bk[:].bitcast(BF16)

            def fn(h):
                ins = None
                for k in range(nblk):
                    ins = h.transpose(out=pb[:, k * 128:(k + 1) * 128], in_=src_ap[:, k * 128:(k + 1) * 128],
                                      identity=ident[:])
                return ins
            S.add("pe", fn, reads=[src_reg, r_ident], writes=[bkr])
            return v3(pb[:, 0:nblk * 128], nblk, 128), bkr

        def hTv(s):
            return v3(hT[:, s % 2], 8, 512)

        keep = {}

        def A_ld(s, c):
            tok0 = s * ST
            if c == 0:
                csb, csr = CS.get()
                sp8 = s % 8
                S.add("sp", lambda h: h.dma_start(out=csb, in_=cstab_d[sp8, :, :]), writes=[csr], stream=CS.sin())
                keep[("cs", s)] = (csb.rearrange("p (t c f) -> p t c f", t=3, c=4, f=64), csr)
            xa, xar = Fp.get()
            S.add("sp", lambda h: h.dma_start(out=xa, in_=x_d[tok0 + c * 128: tok0 + (c + 1) * 128, :]),
                  writes=[xar], stream=Fp.sin())
            keep[("xa", s, c)] = (xa, xar)

        def A_sq(s, c):
            xa, xar = keep.pop(("xa", s, c))
            st, str_ = Stt.get()
            S.add("act", lambda h: h.activation(out=junk[:], in_=xa, func=AF.Square, accum_out=st[:, 0:1]),
                  reads=[xar], writes=[str_, r_junk])
            rstd, rstdr = rms_rstd(st[:, 0:1], str_, D, NORM_EPS)
            keep[("a", s, c)] = (xa, xar, rstd, rstdr)

        def A_cp(s, c):
            xa, xar, rstd, rstdr = keep.pop(("a", s, c))
            hb_, hbr = Hp.get()
            S.add("act", lambda h: h.activation(out=hb_, in_=xa, func=AF.Copy, scale=rstd),
                  reads=[xar, rstdr], writes=[hbr])
            keep[("h", s, c)] = (hb_, hbr)

        def A_tr(s, c):
            hb_, hbr = keep.pop(("h", s, c))
            tp, tpr = transposes(hb_, hbr, 8)
            hv = hTv(s)
            S.add("dve", lambda h: h.tensor_tensor(out=hv[:, :, c * 128:(c + 1) * 128], in0=tp,
                                                   in1=GMIX.unsqueeze(2).to_broadcast([128, 8, 128]), op=ALU.mult),
                  reads=[tpr, r_cpk], writes=[r_hT[s % 2][c]])

        def QK_proj(s, c):
            hv = hTv(s)
            cs4, csr = keep[("cs", s)]
            ta, tar = Fp.get()
            tb, tbr = Fp.get()
            tav = ta.rearrange("p (h t f) -> p h t f", h=8, t=2, f=64)
            tbv = tb.rearrange("p (h t f) -> p h t f", h=8, t=2, f=64)
            cosb = cs4[:, 0, c, :].unsqueeze(1).unsqueeze(1).to_broadcast([128, 4, 2, 64])
            nsin = cs4[:, 2, c, :].unsqueeze(1).to_broadcast([128, 4, 64])
            psin = cs4[:, 1, c, :].unsqueeze(1).to_broadcast([128, 4, 64])
            for j, bb in enumerate((B_Q, B_K)):
                w, wr = blk(s, bb)
                bk, bkr = bank()
                S.add("pe", mm_group(bk[:], [(hv[:, kt, c * 128:(c + 1) * 128], w[:, kt, :]) for kt in range(8)]),
                      reads=[r_hT[s % 2][c], wr], writes=[bkr])
                xv = bk[:].rearrange("p (h t f) -> p h t f", h=4, t=2, f=64)
                hs = slice(4 * j, 4 * j + 4)
                S.add("dve", lambda h, xv=xv, hs=hs: h.tensor_tensor(out=tav[:, hs], in0=xv, in1=cosb, op=ALU.mult),
                      reads=[bkr, csr], writes=[tar])
                S.add("dve", lambda h, xv=xv, hs=hs: h.tensor_tensor(out=tbv[:, hs, 0, :], in0=xv[:, :, 1, :], in1=nsin, op=ALU.mult),
                      reads=[bkr, csr], writes=[tbr])
                S.add("dve", lambda h, xv=xv, hs=hs: h.tensor_tensor(out=tbv[:, hs, 1, :], in0=xv[:, :, 0, :], in1=psin, op=ALU.mult),
                      reads=[bkr, csr], writes=[tbr])
            rot, rotr = Hp.get()
            S.add("dve", lambda h: h.tensor_tensor(out=rot, in0=ta, in1=tb, op=ALU.add), reads=[tar, tbr], writes=[rotr])
            S.add("pool", lambda h: h.tensor_tensor(out=kz[:, c, :], in0=rot[:, 512:1024], in1=ZFm, op=ALU.mult),
                  reads=[rotr, r_cpk], writes=[r_kz[c]])
            keep[("rot", s, c)] = (rot, rotr)

        def QK_tr(s, c):
            rot, rotr = keep.pop(("rot", s, c))
            tp, tpr = transposes(rot, rotr, 8)
            S.add("dve", lambda h: h.tensor_copy(out=qkT[:, :, c * 128:(c + 1) * 128], in_=tp), reads=[tpr], writes=[r_qkT[c]])
            S.add("dve", lambda h: h.tensor_tensor(out=qxT[:, :, c * 128:(c + 1) * 128], in0=tp[:, 0:4, :], in1=XIm, op=ALU.mult),
                  reads=[tpr, r_cpk], writes=[r_qxT[c]])

        def SV(s, c):
            hv = hTv(s)
            y, yr = Fp.get()
            mv, mvr = Stt.get()
            for hf in range(2):
                w, wr = blk(s, B_SV + hf)
                bk, bkr = bank()
                S.add("pe", mm_group(bk[:], [(hv[:, kt, c * 128:(c + 1) * 128], w[:, kt, :]) for kt in range(8)]),
                      reads=[r_hT[s % 2][c], wr], writes=[bkr])
                S.add("act", lambda h, bk=bk, hf=hf: h.activation(out=y[:, hf * 512:(hf + 1) * 512], in_=bk[:], func=AF.Gelu),
                      reads=[bkr], writes=[yr])
                S.add("dve", lambda h, hf=hf: h.bn_stats(out=sv12[:, c, hf * 6:(hf + 1) * 6], in_=y[:, hf * 512:(hf + 1) * 512]),
                      reads=[yr], writes=[r_sv12[c]])
            S.add("dve", lambda h: h.bn_aggr(out=mv[:, 0:2], in_=sv12[:, c, :].rearrange("p (a b) -> p a b", a=2, b=6)),
                  reads=[r_sv12[c]], writes=[mvr])
            S.add("dve", lambda h: h.tensor_scalar(out=mv[:, 2:3], in0=mv[:, 1:2], scalar1=GN_EPS, scalar2=None, op0=ALU.add),
                  reads=[mvr], writes=[mvr])
            S.add("pool", lambda h: h.tensor_tensor(out=mv[:, 3:4], in0=mv[:, 2:3], in1=mh[:, 0:1], op=ALU.pow), reads=[mvr, r_mh], writes=[mvr])
            S.add("dve", lambda h: h.tensor_scalar(out=vbuf[:, c, :], in0=y, scalar1=mv[:, 0:1], scalar2=mv[:, 3:4],
                                                   op0=ALU.subtract, op1=ALU.mult), reads=[yr, mvr], writes=[r_vbuf[c]])

        def SU(s, dt):
            hv = hTv(s)
            w, wr = blk(s, B_SU + dt // 4)
            m = dt % 4
            bk, bkr = bank()
            S.add("pe", mm_group(bk[:], [(w[:, kt, m * 128:(m + 1) * 128], hv[:, kt, :]) for kt in range(8)]),
                  reads=r_hT[s % 2] + [wr], writes=[bkr])
            S.add("act", lambda h: h.activation(out=gu[:, dt, :], in_=bk[:], func=AF.Gelu), reads=[bkr], writes=[r_gu[dt]])

        def SG(s, dt):
            hv = hTv(s)
            w, wr = blk(s, B_SG + dt // 4)
            m = dt % 4
            bk, bkr = bank()
            S.add("pe", mm_group(bk[:], [(w[:, kt, m * 128:(m + 1) * 128], hv[:, kt, :]) for kt in range(8)]),
                  reads=r_hT[s % 2] + [wr], writes=[bkr])
            sl, slr = Bp.get()
            S.add("act", lambda h: h.activation(out=sl, in_=bk[:], func=AF.Silu), reads=[bkr], writes=[slr])
            S.add("dve", lambda h: h.tensor_tensor(out=gu[:, dt, :], in0=gu[:, dt, :], in1=sl, op=ALU.mult),
                  reads=[slr, r_gu[dt]], writes=[r_gu[dt]])

        def MIX(s, dt):
            g = dt // 2
            bk, bkr = bank()

            def fn(h):
                h.matmul(bk[:], lhsT=ones2[0:2, :], rhs=browr[0:2, g, :], start=True, stop=False)
                ins = None
                for c in range(NCH):
                    ins = h.matmul(bk[:, c * 128:(c + 1) * 128], lhsT=vbuf[:, c, dt * 128:(dt + 1) * 128],
                                   rhs=wsT[:, g * 128:(g + 1) * 128], start=False, stop=(c == NCH - 1))
                return ins
            S.add("pe", fn, reads=r_vbuf + [r_wsT, r_ones2, r_brow], writes=[bkr])
            S.add("dve", lambda h: h.tensor_tensor(out=gu[:, dt, :], in0=bk[:], in1=gu[:, dt, :], op=ALU.mult),
                  reads=[bkr, r_gu[dt]], writes=[r_gu[dt]])

        def VRG(s, c, which):
            hv = hTv(s)
            for hf in range(2):
                w, wr = blk(s, (B_V if which == 0 else B_RG) + hf)
                bk, bkr = bank()
                S.add("pe", mm_group(bk[:], [(hv[:, kt, c * 128:(c + 1) * 128], w[:, kt, :]) for kt in range(8)]),
                      reads=[r_hT[s % 2][c], wr], writes=[bkr])
                if which == 0:
                    S.add("act", lambda h, bk=bk, hf=hf: h.activation(out=vbuf[:, c, hf * 512:(hf + 1) * 512], in_=bk[:], func=AF.Copy),
                          reads=[bkr], writes=[r_vbuf[c]])
                else:
                    S.add("act", lambda h, bk=bk, hf=hf: h.activation(out=rg[:, c, hf * 512:(hf + 1) * 512], in_=bk[:], func=AF.Silu),
                          reads=[bkr], writes=[r_rg[c]])

        def RET_s(s, c):
            cs_ = slice(c * 128, (c + 1) * 128)
            bS, bSr = bank()

            def fnS(h):
                ins = None
                for hd in range(4):
                    ins = h.matmul(bS[:, hd * 128:(hd + 1) * 128], lhsT=qkT[:, 4 + hd, cs_], rhs=qkT[:, hd, cs_], start=True, stop=True)
                return ins
            S.add("pe", fnS, reads=[r_qkT[c]], writes=[bSr])
            sdt, sdtr = Bp.get()
            S.add("dve", lambda h: h.tensor_tensor(out=sdt, in0=bS[:], in1=DTm, op=ALU.mult), reads=[bSr, r_cpk], writes=[sdtr])
            keep[("sdt", s, c)] = (sdt, sdtr)

        def RET_a(s, c):
            if s % 8 == 0 and c == 0:
                S.add("pool", lambda h: h.memset(state32[:], 0.0), writes=r_s32)
                S.add("pool", lambda h: h.memset(state_bf[:], 0.0), writes=[r_sbf])
            cs_ = slice(c * 128, (c + 1) * 128)
            sdt, sdtr = keep.pop(("sdt", s, c))
            bR = [bank_fixed(1), bank_fixed(2)] if c % 2 == 0 else [bank_fixed(3), bank_fixed(4)]

            def fnR(h):
                ins = None
                for hd in range(4):
                    o = bR[hd // 2][0][:, (hd % 2) * 256:(hd % 2 + 1) * 256]
                    h.matmul(o, lhsT=sdt[:, hd * 128:(hd + 1) * 128], rhs=vbuf[:, c, hd * 256:(hd + 1) * 256], start=True, stop=False)
                    ins = h.matmul(o, lhsT=qxT[:, hd, cs_], rhs=state_bf[:, hd * 256:(hd + 1) * 256], start=False, stop=True)
                return ins
            S.add("pe", fnR, reads=[sdtr, r_vbuf[c], r_qxT[c], r_sbf], writes=[bR[0][1], bR[1][1]])
            bK = [bank(), bank()]

            def fnK(h):
                ins = None
                for hd in range(4):
                    o = bK[hd // 2][0][:, (hd % 2) * 256:(hd % 2 + 1) * 256]
                    ins = h.matmul(o, lhsT=kz[:, c, hd * 128:(hd + 1) * 128], rhs=vbuf[:, c, hd * 256:(hd + 1) * 256], start=True, stop=True)
                return ins
            S.add("pe", fnK, reads=[r_kz[c], r_vbuf[c]], writes=[bK[0][1], bK[1][1]])
            for hd in range(4):
                S.add("dve", lambda h, hd=hd: h.scalar_tensor_tensor(
                    out=state32[:, hd * 256:(hd + 1) * 256], in0=state32[:, hd * 256:(hd + 1) * 256], scalar=float(decay[hd]),
                    in1=bK[hd // 2][0][:, (hd % 2) * 256:(hd % 2 + 1) * 256], op0=ALU.mult, op1=ALU.add),
                    reads=[r_s32[hd], bK[hd // 2][1]], writes=[r_s32[hd]])
            S.add("dve", lambda h: h.tensor_copy(out=state_bf[:], in_=state32[:]), reads=r_s32, writes=[r_sbf])
            mv, mvr = Stt.get()
            rs, rsr = Stt.get()
            par = c % 2
            for hd in range(4):
                src = bR[hd // 2][0][:, (hd % 2) * 256:(hd % 2 + 1) * 256]
                S.add("dve", lambda h, src=src, hd=hd: h.bn_stats(out=rt24[:, par, hd * 6:(hd + 1) * 6], in_=src),
                      reads=[bR[hd // 2][1]], writes=[r_rt24[par]])
            for hd in range(4):
                S.add("dve", lambda h, hd=hd: h.bn_aggr(out=mv[:, 2 * hd:2 * hd + 2], in_=rt24[:, par, hd * 6:(hd + 1) * 6].unsqueeze(1)),
                      reads=[r_rt24[par]], writes=[mvr])
            mv2 = mv.rearrange("p (h t) -> p h t", h=4, t=2)
            S.add("dve", lambda h: h.tensor_scalar(out=rs[:, 0:4], in0=mv2[:, :, 1], scalar1=GN_EPS, scalar2=None, op0=ALU.add),
                  reads=[mvr], writes=[rsr])
            S.add("pool", lambda h: h.tensor_tensor(out=rs[:, 4:8], in0=rs[:, 0:4], in1=mh[:, 0:4], op=ALU.pow), reads=[rsr, r_mh], writes=[rsr])
            rgs, rgsr = Hp.get()
            for hd in range(4):
                S.add("dve", lambda h, hd=hd: h.tensor_scalar(out=rgs[:, hd * 256:(hd + 1) * 256], in0=rg[:, c, hd * 256:(hd + 1) * 256],
                                                             scalar1=rs[:, 4 + hd:5 + hd], scalar2=None, op0=ALU.mult),
                      reads=[r_rg[c], rsr], writes=[rgsr])
            rn, rnr = Hp.get()
            for hd in range(4):
                src = bR[hd // 2][0][:, (hd % 2) * 256:(hd % 2 + 1) * 256]
                S.add("dve", lambda h, src=src, hd=hd: h.scalar_tensor_tensor(
                    out=rn[:, hd * 256:(hd + 1) * 256], in0=src, scalar=mv[:, 2 * hd:2 * hd + 1], in1=rgs[:, hd * 256:(hd + 1) * 256],
                    op0=ALU.subtract, op1=ALU.mult), reads=[bR[hd // 2][1], mvr, rgsr], writes=[rnr])
            keep[("rn", s, c)] = (rn, rnr)

        def RET_b(s, c):
            rn, rnr = keep.pop(("rn", s, c))
            tp, tpr = transposes(rn, rnr, 8)
            S.add("act", lambda h: h.activation(out=retT[:, :, c * 128:(c + 1) * 128], in_=tp, func=AF.Copy), reads=[tpr], writes=[r_retT[c]])

        def MG(s, which, dt):
            hv = hTv(s)
            bb = (B_MR0, B_MS0, B_MR1, B_MS1)[which + 2 * (dt // 4)]
            w, wr = blk(s, bb)
            m = dt % 4
            bk, bkr = bank()
            S.add("pe", mm_group(bk[:], [(w[:, kt, m * 128:(m + 1) * 128], hv[:, kt, :]) for kt in range(8)]),
                  reads=r_hT[s % 2] + [wr], writes=[bkr])
            if dt < 4:
                sg, sgr = sigl[:, which * 4 + dt, :], r_sigl[which * 4 + dt]
            else:
                slot = (dt - 4) if which == 0 else dt
                sg, sgr = mergedT[:, slot, :], r_mT[slot]
            S.add("act", lambda h: h.activation(out=sg, in_=bk[:], func=AF.Sigmoid), reads=[bkr], writes=[sgr])
            keep[("sig", s, which, dt)] = (sg, sgr)

        def OUTP_S(s, dt):
            m = dt % 4
            wso, wsor = blk(s, B_SO1 if dt >= 4 else B_SO0)
            sb_, sbr = keep.pop(("sig", s, 1, dt))
            bk2, bk2r = bank()
            S.add("pe", mm_group(bk2[:], [(wso[:, kt, m * 128:(m + 1) * 128], gu[:, kt, :]) for kt in range(8)]),
                  reads=r_gu + [wsor], writes=[bk2r])
            b_, br = F2.get()
            S.add("dve", lambda h: h.tensor_tensor(out=b_, in0=bk2[:], in1=sb_, op=ALU.mult), reads=[bk2r, sbr], writes=[br])
            keep[("b", s, dt)] = (b_, br)

        def OUTP_R(s, dt):
            m = dt % 4
            wro, wror = blk(s, B_RO1 if dt >= 4 else B_RO0)
            sa, sar = keep.pop(("sig", s, 0, dt))
            b_, br = keep.pop(("b", s, dt))
            bk1, bk1r = bank()
            S.add("pe", mm_group(bk1[:], [(wro[:, kt, m * 128:(m + 1) * 128], retT[:, kt, :]) for kt in range(8)]),
                  reads=r_retT + [wror], writes=[bk1r])
            a_, ar = F2.get()
            S.add("dve", lambda h: h.tensor_tensor(out=a_, in0=bk1[:], in1=sa, op=ALU.mult), reads=[bk1r, sar], writes=[ar])
            S.add("dve", lambda h: h.tensor_tensor(out=mergedT[:, dt, :], in0=a_, in1=b_, op=ALU.add), reads=[ar, br], writes=[r_mT[dt]])

        def Dst_loadx(s, c):
            t0 = s * ST + c * 128
            pool_ = Xr
            xr, xrr = pool_.get()
            xr_out = pool_.sout()
            S.add("sp", lambda h: h.dma_start(out=xr, in_=x_d[t0:t0 + 128, :]), writes=[xrr], stream=pool_.sin())
            keep[("dx", s, c)] = (xr, xrr, xr_out)

        def Dst_loadp(s, c):
            t0 = s * ST + c * 128
            pb_, pbr = Pb.get()
            S.add("pool", lambda h: h.dma_start(out=pb_, in_=p_d[t0:t0 + 128, :]), writes=[pbr], stream=Pb.sin())
            keep[("dp", s, c)] = (pb_, pbr)

        def Dst(s, c):
            cs_ = slice(c * 128, (c + 1) * 128)
            xr, xrr, xr_out = keep.pop(("dx", s, c))
            pb_, pbr = keep.pop(("dp", s, c))
            x1b, x1br = Hp.get()
            for hf in range(2):
                w, wr = blk(s, B_WO + hf)
                bk, bkr = bank()
                hs = slice(hf * 512, (hf + 1) * 512)
                S.add("pe", mm_group(bk[:], [(mergedT[:, kt, cs_], w[:, kt, :]) for kt in range(8)]), reads=r_mT + [wr], writes=[bkr])
                S.add("dve", lambda h, bk=bk, hs=hs: h.tensor_tensor(out=x1b[:, hs], in0=bk[:], in1=xr[:, hs], op=ALU.add),
                      reads=[bkr, xrr], writes=[x1br])
                S.add("dve", lambda h, bk=bk, hs=hs: h.tensor_tensor(out=xr[:, hs], in0=bk[:], in1=xr[:, hs], op=ALU.add),
                      reads=[bkr, xrr], writes=[xrr])
            st, str_ = Stt.get()
            S.add("act", lambda h: h.activation(out=junk[:], in_=xr, func=AF.Square, accum_out=st[:, 0:1]), reads=[xrr], writes=[str_, r_junk])
            rstd, rstdr = rms_rstd(st[:, 0:1], str_, D, NORM_EPS)
            keep[("d", s, c)] = (xr, xrr, xr_out, pb_, pbr, x1b, x1br, rstd, rstdr)

        def D_tr(s, c):
            xr, xrr, xr_out, pb_, pbr, x1b, x1br, rstd, rstdr = keep.pop(("d", s, c))
            tp, tpr = transposes(x1b, x1br, 8)
            h1t, h1tr = H1T.get()
            h1t3 = v3(h1t, 8, 128)
            S.add("dve", lambda h: h.tensor_tensor(out=h1t3, in0=tp, in1=GPLE.unsqueeze(2).to_broadcast([128, 8, 128]), op=ALU.mult),
                  reads=[tpr, r_cpk], writes=[h1tr])
            if True:
                tpp, tppr = transposes(pb_, pbr, 2)
                pt, ptr = PT.get()
                pt3 = v3(pt, 2, 128)
                S.add("dve", lambda h: h.tensor_copy(out=pt3, in_=tpp), reads=[tppr], writes=[ptr])
                keep[("e", s, c)] = (xr, xrr, xr_out, h1t3, h1tr, pt3, ptr, None, None, rstd, rstdr)
            else:
                keep[("e", s, c)] = (xr, xrr, xr_out, h1t3, h1tr, None, None, pb_, pbr, rstd, rstdr)

        def Est(s, c):
            t0 = s * ST + c * 128
            xr, xrr, xr_out, h1t3, h1tr, pt3, ptr, pb_, pbr, rstd1, rstd1r = keep.pop(("e", s, c))
            if pt3 is None:
                tpp, tppr = transposes(pb_, pbr, 2)
                pt, ptr = PT.get()
                pt3 = v3(pt, 2, 128)
                S.add("dve", lambda h: h.tensor_copy(out=pt3, in_=tpp), reads=[tppr], writes=[ptr])
            for hf in range(2):
                w, wr = blk(s, B_PG + hf)
                bkg, bkgr = bank()
                S.add("pe", mm_group(bkg[:], [(h1t3[:, kt, :], w[:, kt, :]) for kt in range(8)]), reads=[h1tr, wr], writes=[bkgr])
                gt, gtr = F2.get()
                S.add("act", lambda h, gt=gt, bkg=bkg: h.activation(out=gt, in_=bkg[:], func=AF.Sigmoid, scale=rstd1),
                      reads=[bkgr, rstd1r], writes=[gtr])
                bkp, bkpr = bank()
                S.add("pe", mm_group(bkp[:], [(pt3[:, k2, :], wpp[:, k2, hf * 512:(hf + 1) * 512]) for k2 in range(2)]),
                      reads=[ptr, r_wpp], writes=[bkpr])
                S.add("dve", lambda h, gt=gt, bkp=bkp: h.tensor_tensor(out=gt, in0=bkp[:], in1=gt, op=ALU.mult), reads=[bkpr, gtr], writes=[gtr])
                S.add("pool", lambda h, gt=gt, hf=hf: h.tensor_tensor(out=xr[:, hf * 512:(hf + 1) * 512], in0=xr[:, hf * 512:(hf + 1) * 512], in1=gt, op=ALU.add),
                      reads=[xrr, gtr], writes=[xrr])
            keep[("t", s, c)] = (xr, xrr, xr_out)

        def Est_tail(s, c):
            t0 = s * ST + c * 128
            xr, xrr, xr_out = keep.pop(("t", s, c))
            st, str_ = Stt.get()
            S.add("act", lambda h: h.activation(out=junk[:], in_=xr, func=AF.Square, accum_out=st[:, 0:1]), reads=[xrr], writes=[str_, r_junk])
            rstd, rstdr = rms_rstd(st[:, 0:1], str_, D, NORM_EPS)
            S.add("dve", lambda h: h.scalar_tensor_tensor(out=xr, in0=xr, scalar=rstd, in1=GFm, op0=ALU.mult, op1=ALU.mult),
                  reads=[xrr, rstdr, r_cpk], writes=[xrr])
            S.add("pool", lambda h: h.dma_start(out=out_d[t0:t0 + 128, :], in_=xr), reads=[xrr], writes=[], stream=xr_out)

        for c in range(NCH):
            A_ld(0, c)
        S.add("sp", lambda h: h.dma_start(out=cpk[:], in_=cpack_d[:, :]), writes=[r_cpk], stream=mkstream("s_c0"))
        for _ in range(NRING):
            issue_next()
        S.add("pool", lambda h: h.dma_start(out=wpp[:], in_=wpp_d[:, :, :]), writes=[r_wpp], stream=mkstream("s_c3"))
        for c in range(NCH):
            A_sq(0, c)
        for c in range(NCH):
            A_cp(0, c)
        for c in range(NCH):
            A_tr(0, c)
        for s in range(nst):
            nxt = s + 1 < nst
            for c in range(NCH):
                QK_proj(s, c)
            release(); release()
            for c in range(NCH):
                SV(s, c)
            release(); release()
            for dt in range(8):
                SU(s, dt)
                if dt % 4 == 3:
                    release()
            for dt in range(8):
                SG(s, dt)
                if dt % 4 == 3:
                    release()
            for c in range(NCH):
                QK_tr(s, c)
            for dt in range(8):
                MIX(s, dt)
            for c in range(NCH):
                VRG(s, c, 0)
            release(); release()
            for c in range(NCH):
                VRG(s, c, 1)
            release(); release()
            mg_plan = [(0, 0), (1, 0), (0, 4), (1, 4)]
            if nxt:
                for c in range(NCH):
                    A_ld(s + 1, c)
            ret_mode[0] = True
            RET_s(s, 0)
            for c in range(NCH):
                if c + 1 < NCH:
                    RET_s(s, c + 1)
                RET_a(s, c)
                if c >= 1:
                    RET_b(s, c - 1)
                which, d0 = mg_plan[c]
                for m in range(4):
                    MG(s, which, d0 + m)
                release()
            ret_mode[0] = False
            for c in range(NCH):
                Dst_loadx(s, c)
            Dst_loadp(s, 0)
            Dst_loadp(s, 1)
            Dst_loadp(s, 2)
            OUTP_S(s, 4)
            OUTP_S(s, 5)
            RET_b(s, NCH - 1)
            if nxt:
                for c in range(NCH):
                    A_sq(s + 1, c)
            OUTP_R(s, 4)
            OUTP_R(s, 5)
            if nxt:
                for c in range(NCH):
                    A_cp(s + 1, c)
            for i, dt in enumerate((6, 7, 0, 1, 2, 3)):
                OUTP_S(s, dt)
                OUTP_R(s, dt)
                if dt in (7, 3):
                    release(); release()
                if nxt and i in (1, 3, 5):
                    A_tr(s + 1, i // 2)
            if nxt:
                A_tr(s + 1, 3)
            Dst(s, 0)
            Dst(s, 1)
            D_tr(s, 0)
            Dst_loadp(s, 3)
            Dst(s, 2)
            D_tr(s, 1)
            Dst(s, 3)
            release(); release()
            D_tr(s, 2)
            Est(s, 0)
            D_tr(s, 3)
            Est(s, 1)
            Est_tail(s, 0)
            Est(s, 2)
            Est_tail(s, 1)
            Est(s, 3)
            release(); release()
            Est_tail(s, 2)
            Est_tail(s, 3)

        S.assign(engsem)
        build_program.sbuf_left = nc.sbuf_bytes_remaining
        block = E(nc.Block())

        @block.sync
        def _(h):
            S.emit("sp", h, engsem)
            for so in Xr._sout + Fp._sout:
                if so is not None:
                    h.wait_ge(so.sem, so.count)

        @block.gpsimd
        def _(h):
            S.emit("pool", h, engsem)
            for so in Xr._sout + Fp._sout:
                if so is not None:
                    h.wait_ge(so.sem, so.count)

        @block.vector
        def _(h):
            S.emit("dve", h, engsem)

        @block.scalar
        def _(h):
            S.emit("act", h, engsem)

        @block.tensor
        def _(h):
            S.emit("pe", h, engsem)

    return nc


def _module_constants():
    f32, f64 = np.float32, np.float64
    hidx = np.arange(4, dtype=f64)
    log_g = np.log(1.0 - np.power(2.0, -5.0 - hidx))
    idx = np.arange(128, dtype=f64)
    diff = idx[:, None] - idx[None, :]
    scale = 128.0 ** -0.5
    dec = np.where(diff[None] >= 0, np.exp(np.maximum(diff, 0.0)[None] * log_g[:, None, None]), 0.0)
    DT = np.ascontiguousarray((dec * scale).transpose(2, 0, 1)).reshape(128, 512).astype(f32)
    zeta = np.exp((127.0 - idx)[:, None] * log_g[None, :])
    xi = np.exp((idx + 1.0)[:, None] * log_g[None, :]) * scale
    XI = np.broadcast_to(xi.T.reshape(1, 512), (128, 512)).astype(f32)
    ZF = np.repeat(zeta, 128, axis=1).astype(f32)
    decay = np.exp(128.0 * log_g).astype(f32)
    maskT = (idx[None, :] >= idx[:, None]).astype(f32)
    half = 64
    inv = np.power(10000.0, -(np.arange(half, dtype=f64) / half))
    ang = np.arange(SEQ, dtype=f64)[:, None] * inv[None, :]
    cos = np.cos(ang)
    sin = np.sin(ang)
    tab = np.stack([cos, sin, -sin], 0).reshape(3, 8, 4, 128, half)
    cstab = np.ascontiguousarray(tab.transpose(1, 3, 0, 2, 4)).reshape(8, 128, 3 * 4 * half).astype(f32)
    return DT, XI, ZF, decay, maskT, cstab


def _blockify(w, col0):
    return np.ascontiguousarray(w[:, col0:col0 + 512].reshape(8, 128, 512).transpose(1, 0, 2))


_CACHE = {}
_NST = NST


def kernel(x, p, w_in, w_ret_out, w_sgu_out, w_out, sgu_ws, sgu_bs, w_ple_gate, w_ple_proj, g_mixer, g_ple, g_final):
    f32 = np.float32
    x = np.asarray(x, dtype=f32)
    p = np.asarray(p, dtype=f32)
    w_in = np.asarray(w_in, dtype=f32)[0]
    w_ro = np.asarray(w_ret_out, dtype=f32)[0]
    w_so = np.asarray(w_sgu_out, dtype=f32)[0]
    w_o = np.asarray(w_out, dtype=f32)[0]
    w_pg = np.asarray(w_ple_gate, dtype=f32)[0]
    w_pp = np.asarray(w_ple_proj, dtype=f32)[0]
    ws = np.asarray(sgu_ws, dtype=f32)[0]
    bs = np.asarray(sgu_bs, dtype=f32)[0]
    gm = np.asarray(g_mixer, dtype=f32)[0]
    gp = np.asarray(g_ple, dtype=f32)[0]
    gf = np.asarray(g_final, dtype=f32)

    DT, XI, ZF, decay, maskT, cstab = _module_constants()

    wall = np.empty((NBLK, 128, 8, 512), dtype=f32)
    order = [(w_in, OFF_Q), (w_in, OFF_K), (w_in, OFF_SV), (w_in, OFF_SV + 512), (w_in, OFF_SU), (w_in, OFF_SU + 512),
             (w_in, OFF_SG), (w_in, OFF_SG + 512), (w_in, OFF_V), (w_in, OFF_V + 512), (w_in, OFF_RG), (w_in, OFF_RG + 512),
             (w_in, OFF_MR), (w_in, OFF_MS), (w_in, OFF_MR + 512), (w_in, OFF_MS + 512),
             (w_ro, 512), (w_so, 512), (w_ro, 0), (w_so, 0), (w_o, 0), (w_o, 512), (w_pg, 0), (w_pg, 512)]
    for i, (w, c0) in enumerate(order):
        wall[i] = _blockify(w, c0)
    wpp = np.ascontiguousarray(w_pp.reshape(2, 128, 1024).transpose(1, 0, 2))
    cpack = np.empty((128, C_END), dtype=f32)
    cpack[:, C_DT:C_XI] = DT
    cpack[:, C_XI:C_ZF] = XI
    cpack[:, C_ZF:C_GF] = ZF
    cpack[:, C_GF:C_GMIX] = np.broadcast_to(gf[None, :], (128, D))
    wspack = np.empty((128, W_END), dtype=f32)
    wspack[:, W_MASK:W_WST] = maskT
    wspack[:, W_WST:W_END] = ws.transpose(2, 0, 1).reshape(128, 512)
    cpack[:, C_GMIX:C_GPLE] = gm.reshape(8, 128).T
    cpack[:, C_GPLE:C_END] = gp.reshape(8, 128).T
    bs2 = np.ascontiguousarray(np.broadcast_to(bs.reshape(1, 512), (2, 512))).astype(f32)
    ident = np.eye(128, dtype=f32)

    key = ("prog", _NST)
    if key not in _CACHE:
        _CACHE[key] = build_program(decay, nst=_NST)
    nc = _CACHE[key]

    xs = x.reshape(N_CORES, TOK, D)
    ps = p.reshape(N_CORES, TOK, PLE)
    in_maps = []
    for i in range(N_CORES):
        in_maps.append({"x": xs[i], "p": ps[i], "wall": wall, "wpp": wpp, "cpack": cpack, "wspack": wspack, "cstab": cstab,
                        "ident": ident, "bs2": bs2})
    res = run_bass_kernel_spmd(nc, in_maps, core_ids=list(range(N_CORES)))
    out = np.stack([np.asarray(r["out"], dtype=f32) for r in res.results], 0)
    return out.reshape(16, SEQ, D)
```

```python
import numpy as np
from contextlib import ExitStack

import concourse.bass as bass
import concourse.mybir as mybir
from concourse.bass_utils import run_bass_kernel_spmd

F32 = mybir.dt.float32
BF16 = mybir.dt.bfloat16
AF = mybir.ActivationFunctionType
ALU = mybir.AluOpType

N_CORES = 8
D = 1024
SEQ = 4096
TOK = 2 * SEQ
ST = 512
NST = TOK // ST
NCH = ST // 128
PLE = 256
NBLK = 24
NORM_EPS = 1e-6
GN_EPS = 1e-5

OFF_Q, OFF_K, OFF_V, OFF_RG, OFF_SU, OFF_SV, OFF_SG, OFF_MR, OFF_MS = 0, 512, 1024, 2048, 3072, 4096, 5120, 6144, 7168
B_Q, B_K, B_SV, B_SU, B_SG, B_V, B_RG = 0, 1, 2, 4, 6, 8, 10
B_MR0, B_MS0, B_MR1, B_MS1, B_RO1, B_SO1, B_RO0, B_SO0 = 12, 13, 14, 15, 16, 17, 18, 19
B_WO, B_PG = 20, 22

C_DT, C_XI, C_ZF, C_GF, C_GMIX, C_GPLE, C_END = 0, 512, 1024, 1536, 2560, 2568, 2576
W_MASK, W_WST, W_END = 0, 128, 640


class Reg:
    __slots__ = ("name", "w", "rs", "const", "strict")

    def __init__(self, name, const=False, strict=False):
        self.name = name
        self.w = None
        self.rs = []
        self.const = const
        self.strict = strict


class Stream:
    def __init__(self, sem):
        self.sem = sem
        self.count = 0


class Op:
    __slots__ = ("eng", "fn", "deps", "signal", "stream", "ndma", "sig")

    def __init__(self, eng, fn, stream, ndma):
        self.eng = eng
        self.fn = fn
        self.deps = []
        self.signal = False
        self.stream = stream
        self.ndma = ndma
        self.sig = None


class Sched:
    ENGS = ("pe", "act", "dve", "pool", "sp")

    def __init__(self):
        self.ops = {e: [] for e in self.ENGS}

    def add(self, eng, fn, reads=(), writes=(), stream=None, ndma=1):
        op = Op(eng, fn, stream, ndma)
        deps = {}
        for r in reads:
            if r.w is not None:
                deps[r.w] = "RAW"
        for w in writes:
            if w.w is not None:
                deps.setdefault(w.w, "WAW")
            for rd in w.rs:
                deps.setdefault(rd, "WAR")
        deps.pop(op, None)
        unread = set()
        for w in writes:
            if w.strict and w.w is not None and not w.rs:
                unread.add(w.w)
        for d, kind in deps.items():
            if (d.stream is not None or stream is not None or d.eng != eng or kind in ("RAW", "WAR")
                    or (kind == "WAW" and d in unread)):
                op.deps.append(d)
                d.signal = True
        for r in reads:
            if not r.const:
                r.rs.append(op)
        for w in writes:
            w.w = op
            w.rs = []
        self.ops[eng].append(op)
        return op

    def assign(self, engsem):
        cnt = {e: 0 for e in self.ENGS}
        for e in self.ENGS:
            for op in self.ops[e]:
                if op.stream is not None:
                    op.stream.count += 16 * op.ndma
                    op.sig = (op.stream.sem, op.stream.count)
                elif op.signal:
                    cnt[e] += 1
                    op.sig = (engsem[e], cnt[e])

    def emit(self, eng, h, engsem):
        waited = {}
        for op in self.ops[eng]:
            need = {}
            for d in op.deps:
                sem, val = d.sig
                k = id(sem)
                if k not in need or need[k][1] < val:
                    need[k] = (sem, val)
            for k, (sem, val) in need.items():
                if waited.get(k, 0) < val:
                    h.wait_ge(sem, val)
                    waited[k] = val
            ins = op.fn(h)
            if op.stream is not None:
                if not isinstance(ins, (list, tuple)):
                    ins = [ins]
                assert len(ins) == op.ndma
                for i in ins:
                    i.then_inc(op.stream.sem, 16)
            elif op.signal:
                ins.then_inc(engsem[eng], 1)


class Pool:
    def __init__(self, t, n, name, mkstream=None):
        self.t = t
        self.n = n
        self.i = 0
        self.last = 0
        self.name = name
        self.mk = mkstream
        self.regs = [Reg(f"{name}{k}") for k in range(n)]
        self._sin = [None] * n
        self._sout = [None] * n

    def get(self):
        k = self.i
        self.i = (self.i + 1) % self.n
        self.last = k
        return self.t[:, k], self.regs[k]

    def sin(self):
        if self._sin[self.last] is None:
            self._sin[self.last] = self.mk(f"si_{self.name}{self.last}")
        return self._sin[self.last]

    def sout(self):
        if self._sout[self.last] is None:
            self._sout[self.last] = self.mk(f"so_{self.name}{self.last}")
        return self._sout[self.last]


def v3(ap, a, b):
    return ap.rearrange("p (a b) -> p a b", a=a, b=b)


def build_program(decay, nst=NST):
    nc = bass.Bass("TRN2", target_bir_lowering=False)
    S = Sched()

    x_d = nc.dram_tensor("x", [TOK, D], F32, kind="ExternalInput").ap()
    p_d = nc.dram_tensor("p", [TOK, PLE], F32, kind="ExternalInput").ap()
    wall_d = nc.dram_tensor("wall", [NBLK, 128, 8, 512], F32, kind="ExternalInput").ap()
    wpp_d = nc.dram_tensor("wpp", [128, 2, 1024], F32, kind="ExternalInput").ap()
    cpack_d = nc.dram_tensor("cpack", [128, C_END], F32, kind="ExternalInput").ap()
    wspack_d = nc.dram_tensor("wspack", [128, W_END], F32, kind="ExternalInput").ap()
    cstab_d = nc.dram_tensor("cstab", [8, 128, 3 * 4 * 64], F32, kind="ExternalInput").ap()
    ident_d = nc.dram_tensor("ident", [128, 128], F32, kind="ExternalInput").ap()
    bs2_d = nc.dram_tensor("bs2", [2, 512], F32, kind="ExternalInput").ap()
    out_d = nc.dram_tensor("out", [TOK, D], F32, kind="ExternalOutput").ap()
    wsc_d = nc.dram_tensor("wsc", [NBLK, 128, 8, 512], BF16, kind="Internal").ap()

    with ExitStack() as es:
        E = es.enter_context

        def sb(name, shape, dt):
            return E(nc.sbuf_tensor(name, shape, dt))

        def mkstream(name):
            return Stream(E(nc.semaphore(name)))

        cpk = sb("cpk", [128, C_END], F32)
        ident = sb("identb", [128, 128], BF16)
        wsT = sb("wsT", [128, 512], BF16)
        ones2 = sb("ones2", [2, 128], BF16)
        browr = sb("browr", [2, 4, 512], BF16)
        mh = sb("mh", [128, 8], F32)
        wpp = sb("wppb", [128, 2, 1024], BF16)
        ring_t = sb("ring", [128, 4, 8 * 512], BF16)
        NRING = 4
        Fp = Pool(sb("Fp", [128, 4, 1024], F32), 4, "Fp", mkstream)
        F2 = Pool(sb("F2", [128, 4, 512], F32), 4, "F2", mkstream)
        Xr = Pool(sb("Xr", [128, 4, 1024], F32), 4, "Xr", mkstream)
        Hp = Pool(sb("Hp", [128, 5, 1024], BF16), 5, "Hp", mkstream)
        Bp = Pool(sb("Bp", [128, 2, 512], BF16), 2, "Bp", mkstream)
        sigl = sb("sigl", [128, 8, 512], BF16)
        r_sigl = [Reg(f"sigl{i}") for i in range(8)]
        junk = sb("junk", [128, 1024], BF16)
        hT = sb("hT", [128, 2, 8 * 512], BF16)
        vbuf = sb("vbuf", [128, 4, 1024], BF16)
        gu = sb("gu", [128, 8, 512], BF16)
        qkT = sb("qkT", [128, 8, 512], BF16)
        qxT = sb("qxT", [128, 4, 512], BF16)
        kz = sb("kz", [128, 4, 512], BF16)
        rg = sb("rg", [128, 4, 1024], BF16)
        state32 = sb("state32", [128, 1024], F32)
        state_bf = sb("state_bf", [128, 1024], BF16)
        retT = sb("retT", [128, 8, 512], BF16)
        mergedT = sb("mergedT", [128, 8, 512], BF16)
        H1T = Pool(sb("h1T", [128, 3, 1024], BF16), 3, "h1T")
        Pb = Pool(sb("pb", [128, 3, 256], BF16), 3, "pb", mkstream)
        PT = Pool(sb("pT", [128, 3, 256], BF16), 3, "pT")
        CS = Pool(sb("cs", [128, 2, 768], F32), 2, "cs", mkstream)
        Stt = Pool(sb("stat", [128, 28, 8], F32), 28, "stat")
        sv12 = sb("sv12", [128, 4, 12], F32)
        r_sv12 = [Reg(f"sv12_{c}") for c in range(NCH)]
        rt24 = sb("rt24", [128, 2, 24], F32)
        r_rt24 = [Reg("rt24_0"), Reg("rt24_1")]
        banks_t = [E(nc.psum_tensor(f"bank{i}", [128, 512], F32)) for i in range(8)]
        bank_regs = [Reg(f"bank{i}") for i in range(8)]
        bank_i = [0]

        ret_mode = [False]

        def bank():
            if ret_mode[0]:
                k = (0, 5, 6, 7)[bank_i[0] % 4]
            else:
                k = bank_i[0] % 8
            bank_i[0] += 1
            return banks_t[k], bank_regs[k]

        def bank_fixed(k):
            return banks_t[k], bank_regs[k]

        r_cpk = Reg("cpk", const=True)
        r_ident = Reg("ident", const=True)
        r_wsT = Reg("wsT", const=True)
        r_brow = Reg("brow", const=True)
        r_ones2 = Reg("ones2", const=True)
        r_mh = Reg("mh", const=True)
        r_wpp = Reg("wpp", const=True)
        r_junk = Reg("junk", strict=True)
        r_ring = [Reg(f"ring{i}") for i in range(NRING)]
        r_wsc = [[Reg(f"wsc{b}_{q}", const=True) for q in range(4)] for b in range(NBLK)]
        r_hT = [[Reg(f"hT{b}_{c}") for c in range(NCH)] for b in range(2)]
        r_vbuf = [Reg(f"vbuf{c}") for c in range(NCH)]
        r_gu = [Reg(f"gu{d}") for d in range(8)]
        r_qkT = [Reg(f"qkT{c}") for c in range(NCH)]
        r_qxT = [Reg(f"qxT{c}") for c in range(NCH)]
        r_kz = [Reg(f"kz{c}") for c in range(NCH)]
        r_rg = [Reg(f"rg{c}") for c in range(NCH)]
        r_s32 = [Reg(f"state32_{h}") for h in range(4)]
        r_sbf = Reg("state_bf")
        r_retT = [Reg(f"retT{c}") for c in range(NCH)]
        r_mT = [Reg(f"mT{d}") for d in range(8)]

        engsem = {e: E(nc.semaphore(f"sem_{e}")) for e in Sched.ENGS}
        s_ring = [mkstream(f"s_ring{i}") for i in range(NRING)]
        s_ringc = [mkstream(f"s_ringc{i}") for i in range(NRING)]

        cp3 = lambda a, b, n, w: v3(cpk[:, a:b], n, w)
        DTm = cpk[:, C_DT:C_XI]
        XIm = cp3(C_XI, C_ZF, 4, 128)
        ZFm = cpk[:, C_ZF:C_GF]
        GFm = cpk[:, C_GF:C_GMIX]
        GMIX = cpk[:, C_GMIX:C_GPLE]
        GPLE = cpk[:, C_GPLE:C_END]

        bs2_, r_b0 = F2.get()
        hi2f_, r_b2 = F2.get()
        lo2f_, r_b3 = F2.get()
        hi2_, r_b1 = Bp.get()
        bs2, hi2f, lo2f, hi2 = bs2_[0:2, :], hi2f_[0:2, :], lo2f_[0:2, :], hi2_[0:2, :]
        r_bs = [r_b0, r_b1, r_b2, r_b3]
        S.add("sp", lambda h: h.dma_start(out=bs2, in_=bs2_d[:, :]), writes=[r_bs[0]], stream=mkstream("s_c1"))
        S.add("pool", lambda h: h.dma_start(out=ident[:], in_=ident_d[:, :]), writes=[r_ident], stream=mkstream("s_c2"))
        S.add("pool", lambda h: h.memset(ones2[:], 1.0), writes=[r_ones2])
        S.add("pool", lambda h: h.memset(mh[:], -0.5), writes=[r_mh])
        wsp, wspr = Fp.get()
        S.add("sp", lambda h: h.dma_start(out=wsp[:, 0:W_END], in_=wspack_d[:, :]), writes=[wspr], stream=Fp.sin())
        S.add("dve", lambda h: h.tensor_tensor(out=v3(wsT[:], 4, 128), in0=v3(wsp[:, W_WST:W_END], 4, 128),
                                               in1=wsp[:, W_MASK:W_WST].unsqueeze(1).to_broadcast([128, 4, 128]), op=ALU.mult),
              reads=[wspr], writes=[r_wsT])
        S.add("dve", lambda h: h.tensor_copy(out=hi2, in_=bs2), reads=[r_bs[0]], writes=[r_bs[1]])
        S.add("dve", lambda h: h.tensor_copy(out=hi2f, in_=hi2), reads=[r_bs[1]], writes=[r_bs[2]])
        S.add("dve", lambda h: h.tensor_tensor(out=lo2f, in0=bs2, in1=hi2f, op=ALU.subtract),
              reads=[r_bs[0], r_bs[2]], writes=[r_bs[3]])
        br4 = browr[:].rearrange("p g (r i) -> p g r i", r=4, i=128)
        S.add("dve", lambda h: h.tensor_copy(out=br4, in_=v3(lo2f, 4, 128).unsqueeze(2).to_broadcast([2, 4, 4, 128])),
              reads=[r_bs[3]], writes=[r_brow])
        S.add("dve", lambda h: h.tensor_copy(out=br4[0:1], in_=v3(hi2_[0:1, :], 4, 128).unsqueeze(2).to_broadcast([1, 4, 4, 128])),
              reads=[r_bs[1], r_brow], writes=[r_brow])

        ring_live = {}
        load_seq = [0]
        total_loads = nst * NBLK

        def issue_next():
            n = load_seq[0]
            if n >= total_loads:
                return
            load_seq[0] = n + 1
            b = n % NBLK
            k = n % NRING
            if n < NBLK:
                S.add("pool", lambda h, b=b, k=k: h.dma_start(out=v3(ring_t[:, k], 8, 512), in_=wall_d[b, :, :, :]),
                      writes=[r_ring[k]], stream=s_ringc[k])
                S.add("sp", lambda h, b=b, k=k: h.dma_start(out=wsc_d[b, :, :, :], in_=v3(ring_t[:, k], 8, 512)),
                      reads=[r_ring[k]], writes=r_wsc[b], stream=mkstream(f"s_cv{b}"))
            else:
                S.add("sp", lambda h, b=b, k=k: h.dma_start(out=v3(ring_t[:, k], 8, 512), in_=wsc_d[b, :, :, :]),
                      reads=r_wsc[b], writes=[r_ring[k]], stream=s_ring[k])
            ring_live[(n // NBLK, b)] = k

        def blk(s, b):
            k = ring_live[(s, b)]
            return v3(ring_t[:, k], 8, 512), r_ring[k]

        def release():
            issue_next()

        def mm_group(out_ap, pairs):
            def fn(h):
                n = len(pairs)
                ins = None
                for i, (l, r) in enumerate(pairs):
                    ins = h.matmul(out_ap, lhsT=l, rhs=r, start=(i == 0), stop=(i == n - 1))
                return ins
            return fn

        def rms_rstd(ssq_ap, ssq_reg, n, eps):
            st, str_ = Stt.get()
            S.add("pool", lambda h: h.tensor_scalar(out=st[:, 0:1], in0=ssq_ap, scalar1=1.0 / n, scalar2=eps,
                                                    op0=ALU.mult, op1=ALU.add), reads=[ssq_reg], writes=[str_])
            S.add("pool", lambda h: h.tensor_tensor(out=st[:, 1:2], in0=st[:, 0:1], in1=mh[:, 0:1], op=ALU.pow),
                  reads=[str_, r_mh], writes=[str_])
            return st[:, 1:2], str_

        def transposes(src_ap, src_reg, nblk):
            bk, bkr = bank()
            pb = bk[:].bitcast(BF16)

            def fn(h):
                ins = None
                for k in range(nblk):
                    ins = h.transpose(out=pb[:, k * 128:(k + 1) * 128], in_=src_ap[:, k * 128:(k + 1) * 128],
                                      identity=ident[:])
                return ins
            S.add("pe", fn, reads=[src_reg, r_ident], writes=[bkr])
            return v3(pb[:, 0:nblk * 128], nblk, 128), bkr

        def hTv(s):
            return v3(hT[:, s % 2], 8, 512)

        keep = {}

        def A_ld(s, c):
            tok0 = s * ST
            if c == 0:
                csb, csr = CS.get()
                sp8 = s % 8
                S.add("sp", lambda h: h.dma_start(out=csb, in_=cstab_d[sp8, :, :]), writes=[csr], stream=CS.sin())
                keep[("cs", s)] = (csb.rearrange("p (t c f) -> p t c f", t=3, c=4, f=64), csr)
            xa, xar = Fp.get()
            S.add("sp", lambda h: h.dma_start(out=xa, in_=x_d[tok0 + c * 128: tok0 + (c + 1) * 128, :]),
                  writes=[xar], stream=Fp.sin())
            keep[("xa", s, c)] = (xa, xar)

        def A_sq(s, c):
            xa, xar = keep.pop(("xa", s, c))
            st, str_ = Stt.get()
            S.add("act", lambda h: h.activation(out=junk[:], in_=xa, func=AF.Square, accum_out=st[:, 0:1]),
                  reads=[xar], writes=[str_, r_junk])
            rstd, rstdr = rms_rstd(st[:, 0:1], str_, D, NORM_EPS)
            keep[("a", s, c)] = (xa, xar, rstd, rstdr)

        def A_cp(s, c):
            xa, xar, rstd, rstdr = keep.pop(("a", s, c))
            hb_, hbr = Hp.get()
            S.add("act", lambda h: h.activation(out=hb_, in_=xa, func=AF.Copy, scale=rstd),
                  reads=[xar, rstdr], writes=[hbr])
            keep[("h", s, c)] = (hb_, hbr)

        def A_tr(s, c):
            hb_, hbr = keep.pop(("h", s, c))
            tp, tpr = transposes(hb_, hbr, 8)
            hv = hTv(s)
            S.add("dve", lambda h: h.tensor_tensor(out=hv[:, :, c * 128:(c + 1) * 128], in0=tp,
                                                   in1=GMIX.unsqueeze(2).to_broadcast([128, 8, 128]), op=ALU.mult),
                  reads=[tpr, r_cpk], writes=[r_hT[s % 2][c]])

        def QK_proj(s, c):
            hv = hTv(s)
            cs4, csr = keep[("cs", s)]
            ta, tar = Fp.get()
            tb, tbr = Fp.get()
            tav = ta.rearrange("p (h t f) -> p h t f", h=8, t=2, f=64)
            tbv = tb.rearrange("p (h t f) -> p h t f", h=8, t=2, f=64)
            cosb = cs4[:, 0, c, :].unsqueeze(1).unsqueeze(1).to_broadcast([128, 4, 2, 64])
            nsin = cs4[:, 2, c, :].unsqueeze(1).to_broadcast([128, 4, 64])
            psin = cs4[:, 1, c, :].unsqueeze(1).to_broadcast([128, 4, 64])
            for j, bb in enumerate((B_Q, B_K)):
                w, wr = blk(s, bb)
                bk, bkr = bank()
                S.add("pe", mm_group(bk[:], [(hv[:, kt, c * 128:(c + 1) * 128], w[:, kt, :]) for kt in range(8)]),
                      reads=[r_hT[s % 2][c], wr], writes=[bkr])
                xv = bk[:].rearrange("p (h t f) -> p h t f", h=4, t=2, f=64)
                hs = slice(4 * j, 4 * j + 4)
                S.add("dve", lambda h, xv=xv, hs=hs: h.tensor_tensor(out=tav[:, hs], in0=xv, in1=cosb, op=ALU.mult),
                      reads=[bkr, csr], writes=[tar])
                S.add("dve", lambda h, xv=xv, hs=hs: h.tensor_tensor(out=tbv[:, hs, 0, :], in0=xv[:, :, 1, :], in1=nsin, op=ALU.mult),
                      reads=[bkr, csr], writes=[tbr])
                S.add("dve", lambda h, xv=xv, hs=hs: h.tensor_tensor(out=tbv[:, hs, 1, :], in0=xv[:, :, 0, :], in1=psin, op=ALU.mult),
                      reads=[bkr, csr], writes=[tbr])
            rot, rotr = Hp.get()
            S.add("dve", lambda h: h.tensor_tensor(out=rot, in0=ta, in1=tb, op=ALU.add), reads=[tar, tbr], writes=[rotr])
            S.add("pool", lambda h: h.tensor_tensor(out=kz[:, c, :], in0=rot[:, 512:1024], in1=ZFm, op=ALU.mult),
                  reads=[rotr, r_cpk], writes=[r_kz[c]])
            keep[("rot", s, c)] = (rot, rotr)

        def QK_tr(s, c):
            rot, rotr = keep.pop(("rot", s, c))
            tp, tpr = transposes(rot, rotr, 8)
            S.add("dve", lambda h: h.tensor_copy(out=qkT[:, :, c * 128:(c + 1) * 128], in_=tp), reads=[tpr], writes=[r_qkT[c]])
            S.add("dve", lambda h: h.tensor_tensor(out=qxT[:, :, c * 128:(c + 1) * 128], in0=tp[:, 0:4, :], in1=XIm, op=ALU.mult),
                  reads=[tpr, r_cpk], writes=[r_qxT[c]])

        def SV(s, c):
            hv = hTv(s)
            y, yr = Fp.get()
            mv, mvr = Stt.get()
            for hf in range(2):
                w, wr = blk(s, B_SV + hf)
                bk, bkr = bank()
                S.add("pe", mm_group(bk[:], [(hv[:, kt, c * 128:(c + 1) * 128], w[:, kt, :]) for kt in range(8)]),
                      reads=[r_hT[s % 2][c], wr], writes=[bkr])
                S.add("act", lambda h, bk=bk, hf=hf: h.activation(out=y[:, hf * 512:(hf + 1) * 512], in_=bk[:], func=AF.Gelu),
                      reads=[bkr], writes=[yr])
                S.add("dve", lambda h, hf=hf: h.bn_stats(out=sv12[:, c, hf * 6:(hf + 1) * 6], in_=y[:, hf * 512:(hf + 1) * 512]),
                      reads=[yr], writes=[r_sv12[c]])
            S.add("dve", lambda h: h.bn_aggr(out=mv[:, 0:2], in_=sv12[:, c, :].rearrange("p (a b) -> p a b", a=2, b=6)),
                  reads=[r_sv12[c]], writes=[mvr])
            S.add("dve", lambda h: h.tensor_scalar(out=mv[:, 2:3], in0=mv[:, 1:2], scalar1=GN_EPS, scalar2=None, op0=ALU.add),
                  reads=[mvr], writes=[mvr])
            S.add("pool", lambda h: h.tensor_tensor(out=mv[:, 3:4], in0=mv[:, 2:3], in1=mh[:, 0:1], op=ALU.pow), reads=[mvr, r_mh], writes=[mvr])
            S.add("dve", lambda h: h.tensor_scalar(out=vbuf[:, c, :], in0=y, scalar1=mv[:, 0:1], scalar2=mv[:, 3:4],
                                                   op0=ALU.subtract, op1=ALU.mult), reads=[yr, mvr], writes=[r_vbuf[c]])

        def SU(s, dt):
            hv = hTv(s)
            w, wr = blk(s, B_SU + dt // 4)
            m = dt % 4
            bk, bkr = bank()
            S.add("pe", mm_group(bk[:], [(w[:, kt, m * 128:(m + 1) * 128], hv[:, kt, :]) for kt in range(8)]),
                  reads=r_hT[s % 2] + [wr], writes=[bkr])
            S.add("act", lambda h: h.activation(out=gu[:, dt, :], in_=bk[:], func=AF.Gelu), reads=[bkr], writes=[r_gu[dt]])

        def SG(s, dt):
            hv = hTv(s)
            w, wr = blk(s, B_SG + dt // 4)
            m = dt % 4
            bk, bkr = bank()
            S.add("pe", mm_group(bk[:], [(w[:, kt, m * 128:(m + 1) * 128], hv[:, kt, :]) for kt in range(8)]),
                  reads=r_hT[s % 2] + [wr], writes=[bkr])
            sl, slr = Bp.get()
            S.add("act", lambda h: h.activation(out=sl, in_=bk[:], func=AF.Silu), reads=[bkr], writes=[slr])
            S.add("dve", lambda h: h.tensor_tensor(out=gu[:, dt, :], in0=gu[:, dt, :], in1=sl, op=ALU.mult),
                  reads=[slr, r_gu[dt]], writes=[r_gu[dt]])

        def MIX(s, dt):
            g = dt // 2
            bk, bkr = bank()

            def fn(h):
                h.matmul(bk[:], lhsT=ones2[0:2, :], rhs=browr[0:2, g, :], start=True, stop=False)
                ins = None
                for c in range(NCH):
                    ins = h.matmul(bk[:, c * 128:(c + 1) * 128], lhsT=vbuf[:, c, dt * 128:(dt + 1) * 128],
                                   rhs=wsT[:, g * 128:(g + 1) * 128], start=False, stop=(c == NCH - 1))
                return ins
            S.add("pe", fn, reads=r_vbuf + [r_wsT, r_ones2, r_brow], writes=[bkr])
            S.add("dve", lambda h: h.tensor_tensor(out=gu[:, dt, :], in0=bk[:], in1=gu[:, dt, :], op=ALU.mult),
                  reads=[bkr, r_gu[dt]], writes=[r_gu[dt]])

        def VRG(s, c, which):
            hv = hTv(s)
            for hf in range(2):
                w, wr = blk(s, (B_V if which == 0 else B_RG) + hf)
                bk, bkr = bank()
                S.add("pe", mm_group(bk[:], [(hv[:, kt, c * 128:(c + 1) * 128], w[:, kt, :]) for kt in range(8)]),
                      reads=[r_hT[s % 2][c], wr], writes=[bkr])
                if which == 0:
                    S.add("act", lambda h, bk=bk, hf=hf: h.activation(out=vbuf[:, c, hf * 512:(hf + 1) * 512], in_=bk[:], func=AF.Copy),
                          reads=[bkr], writes=[r_vbuf[c]])
                else:
                    S.add("act", lambda h, bk=bk, hf=hf: h.activation(out=rg[:, c, hf * 512:(hf + 1) * 512], in_=bk[:], func=AF.Silu),
                          reads=[bkr], writes=[r_rg[c]])

        def RET_s(s, c):
            cs_ = slice(c * 128, (c + 1) * 128)
            bS, bSr = bank()

            def fnS(h):
                ins = None
                for hd in range(4):
                    ins = h.matmul(bS[:, hd * 128:(hd + 1) * 128], lhsT=qkT[:, 4 + hd, cs_], rhs=qkT[:, hd, cs_], start=True, stop=True)
                return ins
            S.add("pe", fnS, reads=[r_qkT[c]], writes=[bSr])
            sdt, sdtr = Bp.get()
            S.add("dve", lambda h: h.tensor_tensor(out=sdt, in0=bS[:], in1=DTm, op=ALU.mult), reads=[bSr, r_cpk], writes=[sdtr])
            keep[("sdt", s, c)] = (sdt, sdtr)

        def RET_a(s, c):
            if s % 8 == 0 and c == 0:
                S.add("pool", lambda h: h.memset(state32[:], 0.0), writes=r_s32)
                S.add("pool", lambda h: h.memset(state_bf[:], 0.0), writes=[r_sbf])
            cs_ = slice(c * 128, (c + 1) * 128)
            sdt, sdtr = keep.pop(("sdt", s, c))
            bR = [bank_fixed(1), bank_fixed(2)] if c % 2 == 0 else [bank_fixed(3), bank_fixed(4)]

            def fnR(h):
                ins = None
                for hd in range(4):
                    o = bR[hd // 2][0][:, (hd % 2) * 256:(hd % 2 + 1) * 256]
                    h.matmul(o, lhsT=sdt[:, hd * 128:(hd + 1) * 128], rhs=vbuf[:, c, hd * 256:(hd + 1) * 256], start=True, stop=False)
                    ins = h.matmul(o, lhsT=qxT[:, hd, cs_], rhs=state_bf[:, hd * 256:(hd + 1) * 256], start=False, stop=True)
                return ins
            S.add("pe", fnR, reads=[sdtr, r_vbuf[c], r_qxT[c], r_sbf], writes=[bR[0][1], bR[1][1]])
            bK = [bank(), bank()]

            def fnK(h):
                ins = None
                for hd in range(4):
                    o = bK[hd // 2][0][:, (hd % 2) * 256:(hd % 2 + 1) * 256]
                    ins = h.matmul(o, lhsT=kz[:, c, hd * 128:(hd + 1) * 128], rhs=vbuf[:, c, hd * 256:(hd + 1) * 256], start=True, stop=True)
                return ins
            S.add("pe", fnK, reads=[r_kz[c], r_vbuf[c]], writes=[bK[0][1], bK[1][1]])
            for hd in range(4):
                S.add("dve", lambda h, hd=hd: h.scalar_tensor_tensor(
                    out=state32[:, hd * 256:(hd + 1) * 256], in0=state32[:, hd * 256:(hd + 1) * 256], scalar=float(decay[hd]),
                    in1=bK[hd // 2][0][:, (hd % 2) * 256:(hd % 2 + 1) * 256], op0=ALU.mult, op1=ALU.add),
                    reads=[r_s32[hd], bK[hd // 2][1]], writes=[r_s32[hd]])
            S.add("dve", lambda h: h.tensor_copy(out=state_bf[:], in_=state32[:]), reads=r_s32, writes=[r_sbf])
            mv, mvr = Stt.get()
            rs, rsr = Stt.get()
            par = c % 2
            for hd in range(4):
                src = bR[hd // 2][0][:, (hd % 2) * 256:(hd % 2 + 1) * 256]
                S.add("dve", lambda h, src=src, hd=hd: h.bn_stats(out=rt24[:, par, hd * 6:(hd + 1) * 6], in_=src),
                      reads=[bR[hd // 2][1]], writes=[r_rt24[par]])
            for hd in range(4):
                S.add("dve", lambda h, hd=hd: h.bn_aggr(out=mv[:, 2 * hd:2 * hd + 2], in_=rt24[:, par, hd * 6:(hd + 1) * 6].unsqueeze(1)),
                      reads=[r_rt24[par]], writes=[mvr])
            mv2 = mv.rearrange("p (h t) -> p h t", h=4, t=2)
            S.add("dve", lambda h: h.tensor_scalar(out=rs[:, 0:4], in0=mv2[:, :, 1], scalar1=GN_EPS, scalar2=None, op0=ALU.add),
                  reads=[mvr], writes=[rsr])
            S.add("pool", lambda h: h.tensor_tensor(out=rs[:, 4:8], in0=rs[:, 0:4], in1=mh[:, 0:4], op=ALU.pow), reads=[rsr, r_mh], writes=[rsr])
            rgs, rgsr = Hp.get()
            for hd in range(4):
                S.add("dve", lambda h, hd=hd: h.tensor_scalar(out=rgs[:, hd * 256:(hd + 1) * 256], in0=rg[:, c, hd * 256:(hd + 1) * 256],
                                                             scalar1=rs[:, 4 + hd:5 + hd], scalar2=None, op0=ALU.mult),
                      reads=[r_rg[c], rsr], writes=[rgsr])
            rn, rnr = Hp.get()
            for hd in range(4):
                src = bR[hd // 2][0][:, (hd % 2) * 256:(hd % 2 + 1) * 256]
                S.add("dve", lambda h, src=src, hd=hd: h.scalar_tensor_tensor(
                    out=rn[:, hd * 256:(hd + 1) * 256], in0=src, scalar=mv[:, 2 * hd:2 * hd + 1], in1=rgs[:, hd * 256:(hd + 1) * 256],
                    op0=ALU.subtract, op1=ALU.mult), reads=[bR[hd // 2][1], mvr, rgsr], writes=[rnr])
            keep[("rn", s, c)] = (rn, rnr)

        def RET_b(s, c):
            rn, rnr = keep.pop(("rn", s, c))
            tp, tpr = transposes(rn, rnr, 8)
            S.add("act", lambda h: h.activation(out=retT[:, :, c * 128:(c + 1) * 128], in_=tp, func=AF.Copy), reads=[tpr], writes=[r_retT[c]])

        def MG(s, which, dt):
            hv = hTv(s)
            bb = (B_MR0, B_MS0, B_MR1, B_MS1)[which + 2 * (dt // 4)]
            w, wr = blk(s, bb)
            m = dt % 4
            bk, bkr = bank()
            S.add("pe", mm_group(bk[:], [(w[:, kt, m * 128:(m + 1) * 128], hv[:, kt, :]) for kt in range(8)]),
                  reads=r_hT[s % 2] + [wr], writes=[bkr])
            if dt < 4:
                sg, sgr = sigl[:, which * 4 + dt, :], r_sigl[which * 4 + dt]
            else:
                slot = (dt - 4) if which == 0 else dt
                sg, sgr = mergedT[:, slot, :], r_mT[slot]
            S.add("act", lambda h: h.activation(out=sg, in_=bk[:], func=AF.Sigmoid), reads=[bkr], writes=[sgr])
            keep[("sig", s, which, dt)] = (sg, sgr)

        def OUTP_S(s, dt):
            m = dt % 4
            wso, wsor = blk(s, B_SO1 if dt >= 4 else B_SO0)
            sb_, sbr = keep.pop(("sig", s, 1, dt))
            bk2, bk2r = bank()
            S.add("pe", mm_group(bk2[:], [(wso[:, kt, m * 128:(m + 1) * 128], gu[:, kt, :]) for kt in range(8)]),
                  reads=r_gu + [wsor], writes=[bk2r])
            b_, br = F2.get()
            S.add("dve", lambda h: h.tensor_tensor(out=b_, in0=bk2[:], in1=sb_, op=ALU.mult), reads=[bk2r, sbr], writes=[br])
            keep[("b", s, dt)] = (b_, br)

        def OUTP_R(s, dt):
            m = dt % 4
            wro, wror = blk(s, B_RO1 if dt >= 4 else B_RO0)
            sa, sar = keep.pop(("sig", s, 0, dt))
            b_, br = keep.pop(("b", s, dt))
            bk1, bk1r = bank()
            S.add("pe", mm_group(bk1[:], [(wro[:, kt, m * 128:(m + 1) * 128], retT[:, kt, :]) for kt in range(8)]),
                  reads=r_retT + [wror], writes=[bk1r])
            a_, ar = F2.get()
            S.add("dve", lambda h: h.tensor_tensor(out=a_, in0=bk1[:], in1=sa, op=ALU.mult), reads=[bk1r, sar], writes=[ar])
            S.add("dve", lambda h: h.tensor_tensor(out=mergedT[:, dt, :], in0=a_, in1=b_, op=ALU.add), reads=[ar, br], writes=[r_mT[dt]])

        def Dst_loadx(s, c):
            t0 = s * ST + c * 128
            pool_ = Xr
            xr, xrr = pool_.get()
            xr_out = pool_.sout()
            S.add("sp", lambda h: h.dma_start(out=xr, in_=x_d[t0:t0 + 128, :]), writes=[xrr], stream=pool_.sin())
            keep[("dx", s, c)] = (xr, xrr, xr_out)

        def Dst_loadp(s, c):
            t0 = s * ST + c * 128
            pb_, pbr = Pb.get()
            S.add("pool", lambda h: h.dma_start(out=pb_, in_=p_d[t0:t0 + 128, :]), writes=[pbr], stream=Pb.sin())
            keep[("dp", s, c)] = (pb_, pbr)

        def Dst(s, c):
            cs_ = slice(c * 128, (c + 1) * 128)
            xr, xrr, xr_out = keep.pop(("dx", s, c))
            pb_, pbr = keep.pop(("dp", s, c))
            x1b, x1br = Hp.get()
            for hf in range(2):
                w, wr = blk(s, B_WO + hf)
                bk, bkr = bank()
                hs = slice(hf * 512, (hf + 1) * 512)
                S.add("pe", mm_group(bk[:], [(mergedT[:, kt, cs_], w[:, kt, :]) for kt in range(8)]), reads=r_mT + [wr], writes=[bkr])
                S.add("dve", lambda h, bk=bk, hs=hs: h.tensor_tensor(out=x1b[:, hs], in0=bk[:], in1=xr[:, hs], op=ALU.add),
                      reads=[bkr, xrr], writes=[x1br])
                S.add("dve", lambda h, bk=bk, hs=hs: h.tensor_tensor(out=xr[:, hs], in0=bk[:], in1=xr[:, hs], op=ALU.add),
                      reads=[bkr, xrr], writes=[xrr])
            st, str_ = Stt.get()
            S.add("act", lambda h: h.activation(out=junk[:], in_=xr, func=AF.Square, accum_out=st[:, 0:1]), reads=[xrr], writes=[str_, r_junk])
            rstd, rstdr = rms_rstd(st[:, 0:1], str_, D, NORM_EPS)
            keep[("d", s, c)] = (xr, xrr, xr_out, pb_, pbr, x1b, x1br, rstd, rstdr)

        def D_tr(s, c):
            xr, xrr, xr_out, pb_, pbr, x1b, x1br, rstd, rstdr = keep.pop(("d", s, c))
            tp, tpr = transposes(x1b, x1br, 8)
            h1t, h1tr = H1T.get()
            h1t3 = v3(h1t, 8, 128)
            S.add("dve", lambda h: h.tensor_tensor(out=h1t3, in0=tp, in1=GPLE.unsqueeze(2).to_broadcast([128, 8, 128]), op=ALU.mult),
                  reads=[tpr, r_cpk], writes=[h1tr])
            if True:
                tpp, tppr = transposes(pb_, pbr, 2)
                pt, ptr = PT.get()
                pt3 = v3(pt, 2, 128)
                S.add("dve", lambda h: h.tensor_copy(out=pt3, in_=tpp), reads=[tppr], writes=[ptr])
                keep[("e", s, c)] = (xr, xrr, xr_out, h1t3, h1tr, pt3, ptr, None, None, rstd, rstdr)
            else:
                keep[("e", s, c)] = (xr, xrr, xr_out, h1t3, h1tr, None, None, pb_, pbr, rstd, rstdr)

        def Est(s, c):
            t0 = s * ST + c * 128
            xr, xrr, xr_out, h1t3, h1tr, pt3, ptr, pb_, pbr, rstd1, rstd1r = keep.pop(("e", s, c))
            if pt3 is None:
                tpp, tppr = transposes(pb_, pbr, 2)
                pt, ptr = PT.get()
                pt3 = v3(pt, 2, 128)
                S.add("dve", lambda h: h.tensor_copy(out=pt3, in_=tpp), reads=[tppr], writes=[ptr])
            for hf in range(2):
                w, wr = blk(s, B_PG + hf)
                bkg, bkgr = bank()
                S.add("pe", mm_group(bkg[:], [(h1t3[:, kt, :], w[:, kt, :]) for kt in range(8)]), reads=[h1tr, wr], writes=[bkgr])
                gt, gtr = F2.get()
                S.add("act", lambda h, gt=gt, bkg=bkg: h.activation(out=gt, in_=bkg[:], func=AF.Sigmoid, scale=rstd1),
                      reads=[bkgr, rstd1r], writes=[gtr])
                bkp, bkpr = bank()
                S.add("pe", mm_group(bkp[:], [(pt3[:, k2, :], wpp[:, k2, hf * 512:(hf + 1) * 512]) for k2 in range(2)]),
                      reads=[ptr, r_wpp], writes=[bkpr])
                S.add("dve", lambda h, gt=gt, bkp=bkp: h.tensor_tensor(out=gt, in0=bkp[:], in1=gt, op=ALU.mult), reads=[bkpr, gtr], writes=[gtr])
                S.add("pool", lambda h, gt=gt, hf=hf: h.tensor_tensor(out=xr[:, hf * 512:(hf + 1) * 512], in0=xr[:, hf * 512:(hf + 1) * 512], in1=gt, op=ALU.add),
                      reads=[xrr, gtr], writes=[xrr])
            keep[("t", s, c)] = (xr, xrr, xr_out)

        def Est_tail(s, c):
            t0 = s * ST + c * 128
            xr, xrr, xr_out = keep.pop(("t", s, c))
            st, str_ = Stt.get()
            S.add("act", lambda h: h.activation(out=junk[:], in_=xr, func=AF.Square, accum_out=st[:, 0:1]), reads=[xrr], writes=[str_, r_junk])
            rstd, rstdr = rms_rstd(st[:, 0:1], str_, D, NORM_EPS)
            S.add("dve", lambda h: h.scalar_tensor_tensor(out=xr, in0=xr, scalar=rstd, in1=GFm, op0=ALU.mult, op1=ALU.mult),
                  reads=[xrr, rstdr, r_cpk], writes=[xrr])
            S.add("pool", lambda h: h.dma_start(out=out_d[t0:t0 + 128, :], in_=xr), reads=[xrr], writes=[], stream=xr_out)

        for c in range(NCH):
            A_ld(0, c)
        S.add("sp", lambda h: h.dma_start(out=cpk[:], in_=cpack_d[:, :]), writes=[r_cpk], stream=mkstream("s_c0"))
        for _ in range(NRING):
            issue_next()
        S.add("pool", lambda h: h.dma_start(out=wpp[:], in_=wpp_d[:, :, :]), writes=[r_wpp], stream=mkstream("s_c3"))
        for c in range(NCH):
            A_sq(0, c)
        for c in range(NCH):
            A_cp(0, c)
        for c in range(NCH):
            A_tr(0, c)
        for s in range(nst):
            nxt = s + 1 < nst
            for c in range(NCH):
                QK_proj(s, c)
            release(); release()
            for c in range(NCH):
                SV(s, c)
            release(); release()
            for dt in range(8):
                SU(s, dt)
                if dt % 4 == 3:
                    release()
            for dt in range(8):
                SG(s, dt)
                if dt % 4 == 3:
                    release()
            for c in range(NCH):
                QK_tr(s, c)
            for dt in range(8):
                MIX(s, dt)
            for c in range(NCH):
                VRG(s, c, 0)
            release(); release()
            for c in range(NCH):
                VRG(s, c, 1)
            release(); release()
            mg_plan = [(0, 0), (1, 0), (0, 4), (1, 4)]
            if nxt:
                for c in range(NCH):
                    A_ld(s + 1, c)
            ret_mode[0] = True
            RET_s(s, 0)
            for c in range(NCH):
                if c + 1 < NCH:
                    RET_s(s, c + 1)
                RET_a(s, c)
                if c >= 1:
                    RET_b(s, c - 1)
                which, d0 = mg_plan[c]
                for m in range(4):
                    MG(s, which, d0 + m)
                release()
            ret_mode[0] = False
            for c in range(NCH):
                Dst_loadx(s, c)
            Dst_loadp(s, 0)
            Dst_loadp(s, 1)
            Dst_loadp(s, 2)
            OUTP_S(s, 4)
            OUTP_S(s, 5)
            OUTP_S(s, 6)
            RET_b(s, NCH - 1)
            if nxt:
                for c in range(NCH):
                    A_sq(s + 1, c)
            OUTP_R(s, 4)
            OUTP_R(s, 5)
            if nxt:
                for c in range(NCH):
                    A_cp(s + 1, c)
            for i, dt in enumerate((6, 7, 0, 1, 2, 3)):
                if dt != 6:
                    OUTP_S(s, dt)
                OUTP_R(s, dt)
                if dt in (7, 3):
                    release(); release()
                if nxt and i in (1, 3, 5):
                    A_tr(s + 1, i // 2)
            if nxt:
                A_tr(s + 1, 3)
            Dst(s, 0)
            Dst(s, 1)
            D_tr(s, 0)
            Dst_loadp(s, 3)
            Dst(s, 2)
            D_tr(s, 1)
            Dst(s, 3)
            release(); release()
            D_tr(s, 2)
            Est(s, 0)
            D_tr(s, 3)
            Est(s, 1)
            Est_tail(s, 0)
            Est(s, 2)
            Est_tail(s, 1)
            Est(s, 3)
            release(); release()
            Est_tail(s, 2)
            Est_tail(s, 3)

        S.assign(engsem)
        build_program.sbuf_left = nc.sbuf_bytes_remaining
        block = E(nc.Block())

        @block.sync
        def _(h):
            S.emit("sp", h, engsem)
            for so in Xr._sout + Fp._sout:
                if so is not None:
                    h.wait_ge(so.sem, so.count)

        @block.gpsimd
        def _(h):
            S.emit("pool", h, engsem)
            for so in Xr._sout + Fp._sout:
                if so is not None:
                    h.wait_ge(so.sem, so.count)

        @block.vector
        def _(h):
            S.emit("dve", h, engsem)

        @block.scalar
        def _(h):
            S.emit("act", h, engsem)

        @block.tensor
        def _(h):
            S.emit("pe", h, engsem)

    return nc


def _module_constants():
    f32, f64 = np.float32, np.float64
    hidx = np.arange(4, dtype=f64)
    log_g = np.log(1.0 - np.power(2.0, -5.0 - hidx))
    idx = np.arange(128, dtype=f64)
    diff = idx[:, None] - idx[None, :]
    scale = 128.0 ** -0.5
    dec = np.where(diff[None] >= 0, np.exp(np.maximum(diff, 0.0)[None] * log_g[:, None, None]), 0.0)
    DT = np.ascontiguousarray((dec * scale).transpose(2, 0, 1)).reshape(128, 512).astype(f32)
    zeta = np.exp((127.0 - idx)[:, None] * log_g[None, :])
    xi = np.exp((idx + 1.0)[:, None] * log_g[None, :]) * scale
    XI = np.broadcast_to(xi.T.reshape(1, 512), (128, 512)).astype(f32)
    ZF = np.repeat(zeta, 128, axis=1).astype(f32)
    decay = np.exp(128.0 * log_g).astype(f32)
    maskT = (idx[None, :] >= idx[:, None]).astype(f32)
    half = 64
    inv = np.power(10000.0, -(np.arange(half, dtype=f64) / half))
    ang = np.arange(SEQ, dtype=f64)[:, None] * inv[None, :]
    cos = np.cos(ang)
    sin = np.sin(ang)
    tab = np.stack([cos, sin, -sin], 0).reshape(3, 8, 4, 128, half)
    cstab = np.ascontiguousarray(tab.transpose(1, 3, 0, 2, 4)).reshape(8, 128, 3 * 4 * half).astype(f32)
    return DT, XI, ZF, decay, maskT, cstab


def _blockify(w, col0):
    return np.ascontiguousarray(w[:, col0:col0 + 512].reshape(8, 128, 512).transpose(1, 0, 2))


_CACHE = {}
_NST = NST


def kernel(x, p, w_in, w_ret_out, w_sgu_out, w_out, sgu_ws, sgu_bs, w_ple_gate, w_ple_proj, g_mixer, g_ple, g_final):
    f32 = np.float32
    x = np.asarray(x, dtype=f32)
    p = np.asarray(p, dtype=f32)
    w_in = np.asarray(w_in, dtype=f32)[0]
    w_ro = np.asarray(w_ret_out, dtype=f32)[0]
    w_so = np.asarray(w_sgu_out, dtype=f32)[0]
    w_o = np.asarray(w_out, dtype=f32)[0]
    w_pg = np.asarray(w_ple_gate, dtype=f32)[0]
    w_pp = np.asarray(w_ple_proj, dtype=f32)[0]
    ws = np.asarray(sgu_ws, dtype=f32)[0]
    bs = np.asarray(sgu_bs, dtype=f32)[0]
    gm = np.asarray(g_mixer, dtype=f32)[0]
    gp = np.asarray(g_ple, dtype=f32)[0]
    gf = np.asarray(g_final, dtype=f32)

    DT, XI, ZF, decay, maskT, cstab = _module_constants()

    wall = np.empty((NBLK, 128, 8, 512), dtype=f32)
    order = [(w_in, OFF_Q), (w_in, OFF_K), (w_in, OFF_SV), (w_in, OFF_SV + 512), (w_in, OFF_SU), (w_in, OFF_SU + 512),
             (w_in, OFF_SG), (w_in, OFF_SG + 512), (w_in, OFF_V), (w_in, OFF_V + 512), (w_in, OFF_RG), (w_in, OFF_RG + 512),
             (w_in, OFF_MR), (w_in, OFF_MS), (w_in, OFF_MR + 512), (w_in, OFF_MS + 512),
             (w_ro, 512), (w_so, 512), (w_ro, 0), (w_so, 0), (w_o, 0), (w_o, 512), (w_pg, 0), (w_pg, 512)]
    for i, (w, c0) in enumerate(order):
        wall[i] = _blockify(w, c0)
    wpp = np.ascontiguousarray(w_pp.reshape(2, 128, 1024).transpose(1, 0, 2))
    cpack = np.empty((128, C_END), dtype=f32)
    cpack[:, C_DT:C_XI] = DT
    cpack[:, C_XI:C_ZF] = XI
    cpack[:, C_ZF:C_GF] = ZF
    cpack[:, C_GF:C_GMIX] = np.broadcast_to(gf[None, :], (128, D))
    wspack = np.empty((128, W_END), dtype=f32)
    wspack[:, W_MASK:W_WST] = maskT
    wspack[:, W_WST:W_END] = ws.transpose(2, 0, 1).reshape(128, 512)
    cpack[:, C_GMIX:C_GPLE] = gm.reshape(8, 128).T
    cpack[:, C_GPLE:C_END] = gp.reshape(8, 128).T
    bs2 = np.ascontiguousarray(np.broadcast_to(bs.reshape(1, 512), (2, 512))).astype(f32)
    ident = np.eye(128, dtype=f32)

    key = ("prog", _NST)
    if key not in _CACHE:
        _CACHE[key] = build_program(decay, nst=_NST)
    nc = _CACHE[key]

    xs = x.reshape(N_CORES, TOK, D)
    ps = p.reshape(N_CORES, TOK, PLE)
    in_maps = []
    for i in range(N_CORES):
        in_maps.append({"x": xs[i], "p": ps[i], "wall": wall, "wpp": wpp, "cpack": cpack, "wspack": wspack, "cstab": cstab,
                        "ident": ident, "bs2": bs2})
    res = run_bass_kernel_spmd(nc, in_maps, core_ids=list(range(N_CORES)))
    out = np.stack([np.asarray(r["out"], dtype=f32) for r in res.results], 0)
    return out.reshape(16, SEQ, D)
```
